# Optimizing a Trainium2 kernel written in Bass

```python
import math
import jax, jax.numpy as jnp
from jax import lax
import numpy as np

D_MODEL = 1024
BATCH = 8
SEQ = 2048
DEPTH = 1
DEC_BATCH = 2
DEC_SEQ = 16384
PAST_LEN = 128

D_MIX = 2 * D_MODEL
D_ATT = D_MIX // 2
D_SSM = D_MIX - D_ATT
HEAD_DIM = 64
N_ATT_HEADS = D_ATT // HEAD_DIM
ATT_PATTERNS = ((128, 1), (512, 4), (2048, 16))
ATT_BLOCK = 128
N_REL_BUCKETS = 32
REL_MAX_DIST = 1024
SSM_HEAD_DIM = 64
N_SSM_HEADS = D_SSM // SSM_HEAD_DIM
N_SSM_GROUPS = 4
SSM_HEADS_PER_GROUP = N_SSM_HEADS // N_SSM_GROUPS
D_STATE = 128
D_CONV = 5
SSD_CHUNK = 128
CONV_DIM = D_SSM + 2 * N_SSM_GROUPS * D_STATE
IN_DIM = 3 * D_ATT + D_SSM + CONV_DIM + 2 * N_SSM_HEADS
D_FF = 2816
EPS = 1e-6
NEG_INF = -1e30

kernel_name = 'hymba_longnet_ssd_macaron_encoder'


def _rmsnorm(x, g):
    xf = x.astype(jnp.float32)
    y = xf * lax.rsqrt(jnp.mean(xf * xf, axis=-1, keepdims=True) + EPS)
    return (y * g.astype(jnp.float32)).astype(x.dtype)


def _swiglu(x, w_gate, w_up, w_down):
    return (jax.nn.silu(x @ w_gate) * (x @ w_up)) @ w_down


def _t5_bucket(rel):
    nb = N_REL_BUCKETS // 2
    max_exact = nb // 2
    n = np.abs(rel)
    large = max_exact + (np.log(np.maximum(n, 1) / max_exact)
                         / math.log(REL_MAX_DIST / max_exact) * (nb - max_exact)).astype(np.int32)
    large = np.minimum(large, nb - 1)
    return (np.where(rel > 0, nb, 0) + np.where(n < max_exact, n, large)).astype(np.int32)


def _dilated_window_attention(q, k, v, rel_bias, window, dil):
    b, S, h, dh = q.shape
    half = window // (2 * dil)
    L = S // dil
    nblk = -(-L // ATT_BLOCK)
    Lp = nblk * ATT_BLOCK
    kw = ATT_BLOCK + 2 * half
    bd = b * dil

    def to_sub(t):
        return t.reshape(b, L, dil, h, dh).transpose(0, 2, 1, 3, 4).reshape(bd, L, h, dh)

    qs = jnp.pad(to_sub(q), ((0, 0), (0, Lp - L), (0, 0), (0, 0)))
    ks = jnp.pad(to_sub(k), ((0, 0), (half, Lp - L + half), (0, 0), (0, 0)))
    vs = jnp.pad(to_sub(v), ((0, 0), (half, Lp - L + half), (0, 0), (0, 0)))
    key_idx = np.arange(nblk)[:, None] * ATT_BLOCK + np.arange(kw)[None, :]
    kb = ks[:, key_idx]
    vb = vs[:, key_idx]
    qb = qs.reshape(bd, nblk, ATT_BLOCK, h, dh)

    rel_sub = np.arange(kw)[None, :] - half - np.arange(ATT_BLOCK)[:, None]
    bias = jnp.transpose(rel_bias[_t5_bucket(rel_sub * dil)], (2, 0, 1)).astype(jnp.float32)
    in_window = np.abs(rel_sub) <= half
    key_pos = key_idx - half
    valid = in_window[None] & ((key_pos >= 0) & (key_pos < L))[:, None, :]

    logits = jnp.einsum('bnqhd,bnkhd->bhnqk', qb, kb).astype(jnp.float32) * (dh ** -0.5)
    logits = logits + bias[None, :, None]
    logits = jnp.where(valid[None, None], logits, NEG_INF)
    lse = jax.nn.logsumexp(logits, axis=-1)
    p = jnp.exp(logits - lse[..., None])
    o = jnp.einsum('bhnqk,bnkhd->bnqhd', p, vb.astype(jnp.float32))
    o = o.reshape(bd, Lp, h, dh)[:, :L]
    lse = lse.transpose(0, 2, 3, 1).reshape(bd, Lp, h)[:, :L]
    o = o.reshape(b, dil, L, h, dh).transpose(0, 2, 1, 3, 4).reshape(b, S, h, dh)
    lse = lse.reshape(b, dil, L, h).transpose(0, 2, 1, 3).reshape(b, S, h)
    return o, lse


def _dilated_mixture(q, k, v, rel_bias):
    outs, lses = [], []
    for window, dil in ATT_PATTERNS:
        o, lse = _dilated_window_attention(q, k, v, rel_bias, window, dil)
        outs.append(o)
        lses.append(lse)
    wts = jax.nn.softmax(jnp.stack(lses, axis=0), axis=0)
    return jnp.einsum('pbsh,pbshd->bshd', wts, jnp.stack(outs, axis=0))


def _dwconv_centred(x, w, bias):
    pad = D_CONV // 2
    y = lax.conv_general_dilated(x, w[:, None, :].astype(x.dtype), window_strides=(1,),
                                 padding=[(pad, pad)], dimension_numbers=('NWC', 'WIO', 'NWC'),
                                 feature_group_count=x.shape[-1])
    return y + bias


def _ssd_chunked_scan(x, dt, A, Bm, Cm):
    b, S, G, E, P = x.shape
    N = Bm.shape[-1]
    T = SSD_CHUNK
    c = S // T
    a = (dt * A).reshape(b, c, T, G, E)
    xdt = (x.astype(jnp.float32) * dt[..., None]).reshape(b, c, T, G, E, P)
    Bc = Bm.astype(jnp.float32).reshape(b, c, T, G, N)
    Cc = Cm.astype(jnp.float32).reshape(b, c, T, G, N)
    a_cum = jnp.cumsum(a, axis=2)
    lower = np.tril(np.ones((T, T), dtype=bool))[None, None, :, :, None, None]
    seg = a_cum[:, :, :, None] - a_cum[:, :, None, :]
    decay_intra = jnp.where(lower, jnp.exp(jnp.where(lower, seg, 0.0)), 0.0)
    cb = jnp.einsum('bclgn,bcsgn->bclsg', Cc, Bc)
    y_diag = jnp.einsum('bclsg,bclsge,bcsgep->bclgep', cb, decay_intra, xdt)
    decay_to_end = jnp.exp(a_cum[:, :, -1:] - a_cum)
    chunk_states = jnp.einsum('bcsgn,bcsge,bcsgep->bcgepn', Bc, decay_to_end, xdt)
    chunk_decay = jnp.exp(a_cum[:, :, -1])

    def step(hst, inp):
        dec, st = inp
        return hst * dec[..., None, None] + st, hst

    h0 = jnp.zeros((b, G, E, P, N), jnp.float32)
    _, h_prev = lax.scan(step, h0, (jnp.moveaxis(chunk_decay, 1, 0), jnp.moveaxis(chunk_states, 1, 0)))
    h_prev = jnp.moveaxis(h_prev, 0, 1)
    y_off = jnp.einsum('bclgn,bcgepn,bclge->bclgep', Cc, h_prev, jnp.exp(a_cum))
    return (y_diag + y_off).reshape(b, S, G, E, P)


def _ssd_bidirectional(xs, dt_raw, dt_bias, a_log, d_skip, Bm, Cm):
    b, S = xs.shape[:2]
    G, E = N_SSM_GROUPS, SSM_HEADS_PER_GROUP
    dt = jax.nn.softplus(dt_raw.astype(jnp.float32) + dt_bias.astype(jnp.float32))
    A = -jnp.exp(a_log.astype(jnp.float32))
    xg = xs.reshape(b, S, G, E, SSM_HEAD_DIM)
    flip = lambda t: jnp.flip(t, axis=1)
    y_f = _ssd_chunked_scan(xg, dt[:, :, 0].reshape(b, S, G, E), A[0].reshape(G, E), Bm, Cm)
    y_b = flip(_ssd_chunked_scan(flip(xg), flip(dt[:, :, 1]).reshape(b, S, G, E),
                                 A[1].reshape(G, E), flip(Bm), flip(Cm)))
    y = y_f + y_b + d_skip.astype(jnp.float32).reshape(G, E)[:, :, None] * xg.astype(jnp.float32)
    return y.reshape(b, S, D_SSM)


def _layer(x, rel_bias, ffn1_norm_g, ffn1_w_gate, ffn1_w_up, ffn1_w_down, mix_norm_g, w_in,
           q_norm_g, k_norm_g, attn_out_g, conv_w, conv_b, dt_bias, a_log, d_skip, ssm_out_g,
           w_out, ffn2_norm_g, ffn2_w_gate, ffn2_w_up, ffn2_w_down):
    b, S, _ = x.shape
    x = x + (0.5 * _swiglu(_rmsnorm(x, ffn1_norm_g), ffn1_w_gate, ffn1_w_up, ffn1_w_down)).astype(x.dtype)
    hn = _rmsnorm(x, mix_norm_g)
    proj = hn @ w_in
    q, k, v, z, xbc, dt_raw = jnp.split(
        proj, [D_ATT, 2 * D_ATT, 3 * D_ATT, 3 * D_ATT + D_SSM, 3 * D_ATT + D_SSM + CONV_DIM], axis=-1)
    q = _rmsnorm(q.reshape(b, S, N_ATT_HEADS, HEAD_DIM), q_norm_g)
    k = _rmsnorm(k.reshape(b, S, N_ATT_HEADS, HEAD_DIM), k_norm_g)
    v = v.reshape(b, S, N_ATT_HEADS, HEAD_DIM)
    attn = _dilated_mixture(q, k, v, rel_bias).reshape(b, S, D_ATT)
    attn = _rmsnorm(attn, attn_out_g)
    xbc = jax.nn.silu(_dwconv_centred(xbc, conv_w, conv_b))
    xs, Bm, Cm = jnp.split(xbc, [D_SSM, D_SSM + N_SSM_GROUPS * D_STATE], axis=-1)
    y = _ssd_bidirectional(xs.reshape(b, S, N_SSM_HEADS, SSM_HEAD_DIM),
                           dt_raw.reshape(b, S, 2, N_SSM_HEADS), dt_bias, a_log, d_skip,
                           Bm.reshape(b, S, N_SSM_GROUPS, D_STATE), Cm.reshape(b, S, N_SSM_GROUPS, D_STATE))
    y = y * jax.nn.silu(z.astype(jnp.float32))
    gsz = D_SSM // N_SSM_GROUPS
    y = _rmsnorm(y.reshape(b, S, N_SSM_GROUPS, gsz), ssm_out_g.reshape(N_SSM_GROUPS, gsz)).reshape(b, S, D_SSM)
    mix = jnp.concatenate([attn, y], axis=-1)
    x = x + (mix @ w_out).astype(x.dtype)
    x = x + (0.5 * _swiglu(_rmsnorm(x, ffn2_norm_g), ffn2_w_gate, ffn2_w_up, ffn2_w_down)).astype(x.dtype)
    return x


def setup_inputs(seed: int = 0) -> dict:
    key = jax.random.key(seed)
    ks = jax.random.split(key, 24)
    f32 = jnp.float32
    nrm = lambda k, shape, s: jax.random.normal(k, shape, f32) * s
    gain = lambda k, shape: 1.0 + 0.02 * jax.random.normal(k, shape, f32)
    dt0 = jnp.exp(jax.random.uniform(ks[14], (DEPTH, 2, N_SSM_HEADS), f32)
                  * (math.log(0.1) - math.log(0.001)) + math.log(0.001))
    return {
        'x_prompt': jax.random.normal(ks[0], (BATCH, SEQ, D_MODEL), f32),
        'x_sample': jax.random.normal(ks[1], (DEC_BATCH, DEC_SEQ, D_MODEL), f32),
        'rel_bias': nrm(ks[2], (N_REL_BUCKETS, N_ATT_HEADS), 0.5),
        'ffn1_norm_g': gain(ks[3], (DEPTH, D_MODEL)),
        'ffn1_w_gate': nrm(ks[4], (DEPTH, D_MODEL, D_FF), D_MODEL ** -0.5),
        'ffn1_w_up': nrm(ks[5], (DEPTH, D_MODEL, D_FF), D_MODEL ** -0.5),
        'ffn1_w_down': nrm(ks[6], (DEPTH, D_FF, D_MODEL), D_FF ** -0.5),
        'mix_norm_g': gain(ks[7], (DEPTH, D_MODEL)),
        'w_in': nrm(ks[8], (DEPTH, D_MODEL, IN_DIM), D_MODEL ** -0.5),
        'q_norm_g': gain(ks[9], (DEPTH, HEAD_DIM)),
        'k_norm_g': gain(ks[10], (DEPTH, HEAD_DIM)),
        'attn_out_g': gain(ks[11], (DEPTH, D_ATT)),
        'conv_w': nrm(ks[12], (DEPTH, D_CONV, CONV_DIM), D_CONV ** -0.5),
        'conv_b': nrm(ks[13], (DEPTH, CONV_DIM), 0.01),
        'dt_bias': dt0 + jnp.log(-jnp.expm1(-dt0)),
        'a_log': jnp.log(jax.random.uniform(ks[15], (DEPTH, 2, N_SSM_HEADS), f32, 1.0, 16.0)),
        'd_skip': gain(ks[16], (DEPTH, N_SSM_HEADS)),
        'ssm_out_g': gain(ks[17], (DEPTH, D_SSM)),
        'w_out': nrm(ks[18], (DEPTH, D_MIX, D_MODEL), D_MIX ** -0.5),
        'ffn2_norm_g': gain(ks[19], (DEPTH, D_MODEL)),
        'ffn2_w_gate': nrm(ks[20], (DEPTH, D_MODEL, D_FF), D_MODEL ** -0.5),
        'ffn2_w_up': nrm(ks[21], (DEPTH, D_MODEL, D_FF), D_MODEL ** -0.5),
        'ffn2_w_down': nrm(ks[22], (DEPTH, D_FF, D_MODEL), D_FF ** -0.5),
    }


def reference(x_prompt, x_sample, rel_bias, ffn1_norm_g, ffn1_w_gate, ffn1_w_up, ffn1_w_down,
              mix_norm_g, w_in, q_norm_g, k_norm_g, attn_out_g, conv_w, conv_b, dt_bias, a_log,
              d_skip, ssm_out_g, w_out, ffn2_norm_g, ffn2_w_gate, ffn2_w_up, ffn2_w_down):
    def trunk(x):
        for l in range(DEPTH):
            x = _layer(x, rel_bias, ffn1_norm_g[l], ffn1_w_gate[l], ffn1_w_up[l], ffn1_w_down[l],
                       mix_norm_g[l], w_in[l], q_norm_g[l], k_norm_g[l], attn_out_g[l], conv_w[l],
                       conv_b[l], dt_bias[l], a_log[l], d_skip[l], ssm_out_g[l], w_out[l],
                       ffn2_norm_g[l], ffn2_w_gate[l], ffn2_w_up[l], ffn2_w_down[l])
        return x

    y_prompt = trunk(x_prompt)
    y_sample = trunk(x_sample)
    return (y_prompt, y_sample)
```

```python
import numpy as np
from contextlib import ExitStack
import ml_dtypes
import concourse.bass as bass
import concourse.mybir as mybir
from concourse.bass_utils import run_bass_kernel_spmd

F32 = mybir.dt.float32
BF16 = mybir.dt.bfloat16
AF = mybir.ActivationFunctionType
ALU = mybir.AluOpType
AX = mybir.AxisListType

COMPUTE = ("pe", "act", "dve", "pool")

D = 1024
FF = 2816
MC = FF // 128
IN_DIM = 6176
NPS = 2048
HALO = 1024
TS = 4096
NSS = TS + 2 * HALO
NTOK = NPS + NSS
NOWN = NPS + TS
EPS = 1e-6
NEG = -30000.0
GT = 512
KPAD = 1024
VW = 16 * 65

V_G1, V_GMIX, V_G2, V_GATT, V_GSSM = 0, 8, 16, 24, 32
V_GQ, V_GK = 40, 41
V_CW = 42
V_CB = 122
V_DTB = 138
V_ALOG = 170
V_DSK = 202
NV = 218
C_ID, C_TRIF, C_TRIB, C_BLK, C_ANTI, C_ONES = 0, 128, 256, 384, 512, 640
NCONST = 768
def _kv_layout():
    cols = []
    for seg, T in (("P", NPS), ("S", TS)):
        for p, dil in enumerate((1, 4, 16)):
            nblk = T // (128 * dil)
            for r in range(dil):
                for jt in range(nblk + 1):
                    cols.append((seg, dil, r, jt))
    return cols
KV_COLS = _kv_layout()
NKV = len(KV_COLS)


class Op:
    __slots__ = ("eng", "fn", "deps", "idx", "signal", "dma_chan", "cnt", "inc")

    def __init__(self, eng, fn, idx, dma_chan=None, inc=16):
        self.eng = eng
        self.inc = inc
        self.fn = fn
        self.idx = idx
        self.deps = set()
        self.signal = False
        self.dma_chan = dma_chan
        self.cnt = 0


class Prog:
    def __init__(self, nc):
        self.nc = nc
        self.ops = []
        self.last_w = {}
        self.readers = {}
        self.barrier_deps = set()
        self.last_eng = {}
        self.last_chan = {}

    mute = False

    def op(self, eng, fn, reads=(), writes=(), chan=None, inc=16):
        if self.mute:
            return None
        o = Op(eng, fn, len(self.ops), dma_chan=chan, inc=inc)
        self.ops.append(o)
        o.deps |= self.barrier_deps
        for r in reads:
            w = self.last_w.get(r)
            if w is not None:
                o.deps.add(w)
        for r in writes:
            w = self.last_w.get(r)
            if w is not None:
                o.deps.add(w)
            for rd in self.readers.get(r, ()):
                o.deps.add(rd)
        for r in reads:
            self.readers.setdefault(r, []).append(o.idx)
        for r in writes:
            self.last_w[r] = o.idx
            self.readers[r] = []
        o.deps.discard(o.idx)
        if chan is not None:
            self.last_chan[chan] = o.idx
        else:
            self.last_eng[eng] = o.idx
        return o

    def barrier(self):
        self.barrier_deps = set(self.last_eng.values()) | set(self.last_chan.values())

    def dma(self, out, in_, reads, writes, chan, q="sp"):
        return self.op(q, lambda e: e.dma_start(out=out, in_=in_), reads, writes, chan=chan)

    def build(self, sems):
        ops = self.ops
        eng_sem = {e: sems[e] for e in COMPUTE}
        chan_sem = {}
        free = list(sems["dma"])
        chan_cnt = {}
        for o in ops:
            if o.dma_chan is not None:
                if o.dma_chan not in chan_sem:
                    chan_sem[o.dma_chan] = free.pop()
                chan_cnt[o.dma_chan] = chan_cnt.get(o.dma_chan, 0) + o.inc
                o.cnt = chan_cnt[o.dma_chan]
        waited = {}
        need = []
        for o in ops:
            keep = []
            for d in sorted(o.deps):
                od = ops[d]
                if od.dma_chan is not None:
                    key = (o.eng, "c", od.dma_chan)
                    if waited.get(key, -1) >= od.cnt:
                        continue
                    waited[key] = od.cnt
                    keep.append(d)
                else:
                    if od.eng == o.eng and o.eng == "pe" and o.dma_chan is None:
                        continue
                    key = (o.eng, "e", od.eng)
                    if waited.get(key, -1) >= d:
                        continue
                    waited[key] = d
                    keep.append(d)
                    od.signal = True
            need.append(keep)
        cnt = {e: 0 for e in COMPUTE}
        for o in ops:
            if o.dma_chan is None and o.signal:
                cnt[o.eng] += 1
                o.cnt = cnt[o.eng]
        streams = {}
        for o, keep in zip(ops, need):
            waits = {}
            for d in keep:
                od = ops[d]
                s = chan_sem[od.dma_chan] if od.dma_chan is not None else eng_sem[od.eng]
                k = id(s)
                if k not in waits or waits[k][1] < od.cnt:
                    waits[k] = (s, od.cnt)
            streams.setdefault(o.eng, []).append((o, list(waits.values())))
        self.streams = streams
        self.chan_sem = chan_sem
        self.chan_cnt = chan_cnt
        self.eng_sem = eng_sem

    def emit(self, block, out_chans=()):
        engmap = {"pe": block.tensor, "act": block.scalar, "dve": block.vector,
                  "pool": block.gpsimd, "sp": block.sync}
        streams = self.streams
        chan_sem, chan_cnt, eng_sem = self.chan_sem, self.chan_cnt, self.eng_sem

        def make(ename):
            lst = streams.get(ename, [])

            def body(e):
                for o, waits in lst:
                    for s, c in waits:
                        e.wait_ge(s, c)
                    ins = o.fn(e)
                    if o.dma_chan is not None:
                        ins.then_inc(chan_sem[o.dma_chan], o.inc)
                    elif o.signal:
                        ins.then_inc(eng_sem[o.eng], 1)
                if ename == "sp":
                    for ch in out_chans:
                        if ch in chan_sem:
                            e.wait_ge(chan_sem[ch], chan_cnt[ch])
            return body

        for ename in ("sp", "pe", "act", "dve", "pool"):
            if ename == "sp" or streams.get(ename):
                engmap[ename](make(ename))


class Rot:
    def __init__(self, items):
        self.items = items
        self.i = 0

    def next(self):
        it = self.items[self.i % len(self.items)]
        self.i += 1
        return it


def build_program(debug=(), stop_after="C"):
    nc = bass.Bass("TRN2", target_bir_lowering=False)
    es = ExitStack()

    def din(name, shape, dt=F32):
        return nc.dram_tensor(name, list(shape), dt, kind="ExternalInput").ap()

    def dscr(name, shape, dt):
        if name in debug:
            return nc.dram_tensor(name, list(shape), dt, kind="ExternalOutput").ap()
        return nc.dram_tensor(name, list(shape), dt).ap()

    xin = din("xin", [NTOK, D])
    w_src = {
        "w1g": din("w1g", [D, FF]), "w1u": din("w1u", [D, FF]), "w1d": din("w1d", [FF, D]),
        "win": din("win", [D, IN_DIM]), "wout": din("wout", [2 * D, D]),
        "w2g": din("w2g", [D, FF]), "w2u": din("w2u", [D, FF]), "w2d": din("w2d", [FF, D]),
    }
    vecs_d = din("vecs", [128, NV])
    consts_d = din("consts", [128, NCONST])
    yout = nc.dram_tensor("yout", [NOWN, D], F32, kind="ExternalOutput").ap()

    wb = {
        "w1g": dscr("wb1g", [128, 8, FF], BF16), "w1u": dscr("wb1u", [128, 8, FF], BF16),
        "w1d": dscr("wb1d", [128, MC, D], BF16), "win": dscr("wbin", [128, 8, IN_DIM], BF16),
        "wout": dscr("wbout", [128, 16, D], BF16),
        "w2g": dscr("wb2g", [128, 8, FF], BF16), "w2u": dscr("wb2u", [128, 8, FF], BF16),
        "w2d": dscr("wb2d", [128, MC, D], BF16),
    }
    x1s = dscr("x1s", [NTOK, D], F32)
    qT = dscr("qT", [D, NTOK], BF16)
    kT_full = dscr("kT", [D, KPAD + NTOK], BF16)
    kT = kT_full[:, KPAD:]
    vtok_full = dscr("vtok", [KPAD + NTOK, VW], BF16)
    vtok = vtok_full[KPAD:, :]
    ztok = dscr("ztok", [NTOK, D], BF16)
    dtr = dscr("dtr", [NTOK, 32], F32)
    xbcT = dscr("xbcT", [2 * D, NTOK], BF16)

    def sb(name, shape, dt):
        return es.enter_context(nc.sbuf_tensor("sb_" + name, list(shape), dt))

    sems = {e: es.enter_context(nc.semaphore("s_" + e)) for e in COMPUTE}
    P = Prog(nc)

    banks = [es.enter_context(nc.psum_tensor(f"bank{i}", [128, 512], F32)) for i in range(8)]

    def bk(i):
        return banks[i], ("ps", i)

    vecs = sb("vecs", [128, NV], F32)
    cst = sb("cst", [128, NCONST], F32)
    cstb = sb("cstb", [128, NCONST], BF16)
    P.dma(vecs[:], vecs_d, [], ["vecs"], "ld_vecs")
    P.dma(cst[:], consts_d, [], ["cst"], "ld_cst")
    P.op("dve", lambda e: e.tensor_copy(out=cstb[:], in_=cst[:]), ["cst"], ["cstb"])
    ident_b = cstb[:, C_ID:C_ID + 128]
    blk_b = cstb[:, C_BLK:C_BLK + 128]

    import os as _os
    if _os.environ.get("SKIP_PRE"):
        P.mute = True
    with ExitStack() as ph:
        def psb(name, shape, dt):
            return ph.enter_context(nc.sbuf_tensor("sb_" + name, list(shape), dt))
        CW = 2048
        st32 = [psb(f"st32_{i}", [128, CW], F32) for i in range(4)]
        st16 = [psb(f"st16_{i}", [128, CW], BF16) for i in range(4)]
        zt = psb("zt", [128, 8, VW], BF16)
        P.op("pool", lambda e: e.memset(zt[:], 0.0), [], ["zt"])
        P.dma(kT_full[:, 0:KPAD].rearrange("(c p) t -> p c t", p=128), zt[:, :, 0:1024], ["zt"], ["kpad"], "st_zt", q="pool")
        P.dma(vtok_full[0:KPAD, :].rearrange("(t p) c -> p t c", p=128), zt[:], ["zt"], ["vpad"], "st_zt", q="pool")
        folds = {"w1g": V_G1, "w1u": V_G1, "win": V_GMIX, "w2g": V_G2, "w2u": V_G2, "wout": V_GATT}
        slot = 0
        for wname, src in w_src.items():
            if wname not in ("w1g", "w1u"):
                continue
            K, N = src.shape
            for kc in range(K // 128):
                for c0 in range(0, N, CW):
                    c1 = min(N, c0 + CW)
                    s = slot % 4
                    slot += 1
                    a32, a16 = st32[s], st16[s]
                    P.dma(a32[:, 0:c1 - c0], src[kc * 128:(kc + 1) * 128, c0:c1], [], [f"st32_{s}"], f"ldw{s}")
                    fold = folds.get(wname)
                    eng = ("dve", "act")[s % 2]
                    if fold is None:
                        if eng == "act":
                            P.op(eng, lambda e, a16=a16, a32=a32, n=c1 - c0: e.copy(out=a16[:, 0:n], in_=a32[:, 0:n]),
                                 [f"st32_{s}"], [f"st16_{s}"])
                        else:
                            P.op(eng, lambda e, a16=a16, a32=a32, n=c1 - c0: e.tensor_copy(out=a16[:, 0:n], in_=a32[:, 0:n]),
                                 [f"st32_{s}"], [f"st16_{s}"])
                    else:
                        col = vecs[:, fold + kc:fold + kc + 1]
                        if eng == "act":
                            P.op(eng, lambda e, a16=a16, a32=a32, n=c1 - c0, col=col: e.activation(
                                out=a16[:, 0:n], in_=a32[:, 0:n], func=AF.Copy, scale=col),
                                [f"st32_{s}", "vecs"], [f"st16_{s}"])
                        else:
                            P.op(eng, lambda e, a16=a16, a32=a32, n=c1 - c0, col=col: e.tensor_scalar(
                                out=a16[:, 0:n], in0=a32[:, 0:n], scalar1=col, scalar2=None, op0=ALU.mult),
                                [f"st32_{s}", "vecs"], [f"st16_{s}"])
                    P.dma(wb[wname][:, kc, c0:c1], a16[:, 0:c1 - c0], [f"st16_{s}"], [("wb", wname)], f"stw{s}")
        P.barrier()
    if stop_after == "W":
        return finish(nc, es, P, sems, ["stw0", "stw1", "stw2"])

    def ffn_phase(ph, groups, wg, wu, wd, gcol_unused, x_src_fn, after_ffn_fn, tagp):
        pass

    from types import SimpleNamespace

    def ffn_ctx(ph, tag):
        def psb(name, shape, dt):
            return ph.enter_context(nc.sbuf_tensor("sb_" + tag + name, list(shape), dt))
        NWB = 6
        wbufs = Rot([(psb(f"wbuf{i}", [128, 8, 512], BF16), f"wbuf{i}") for i in range(NWB)])
        xgs = Rot([(psb(f"xg{i}", [128, 4, D], F32), f"xg{i}") for i in range(2)])
        xn = psb("xn", [128, 4, D], BF16)
        xnT = psb("xnT", [128, 8, GT], BF16)
        xnT2 = psb("xnT2", [128, 8, GT], BF16) if tag == "a" else None
        hT = psb("hT", [128, MC, GT], BF16)
        sils = Rot([(psb(f"sil{i}", [128, GT], F32), f"sil{i}") for i in range(2)])
        ssq = psb("ssq", [128, 8], F32)
        rstd = psb("rstd", [128, 8], F32)
        junk = psb("junk", [128, D], BF16)
        sqs = Rot([(psb(f"sq{i}", [128, GT], BF16), f"sq{i}") for i in range(2)])
        lns = Rot([(psb(f"ln{i}", [128, GT], F32), f"ln{i}") for i in range(2)])
        qns = Rot([(psb(f"qn{i}", [128, GT], BF16), f"qn{i}") for i in range(3)])
        vst = psb("vst", [128, 4, 16, 65], BF16)
        if tag == "a":
            P.op("pool", lambda e: e.memset(vst[:, :, :, 64:65], 1.0), [], ["vst"])
        zst = psb("zst", [128, 4, D], BF16)
        dst = psb("dst", [128, 4, 32], F32)
        tr_banks = Rot([bk(0), bk(1)])
        g_banks = Rot([bk(2), bk(3)])
        u_banks = Rot([bk(4), bk(5)])
        d_banks = Rot([bk(6), bk(7)])
        fm_banks = Rot([bk(2), bk(3), bk(4), bk(5)])
        ev = Rot(["dve", "act"])

        def wload(src_ap, shape_view, rkey=None):
            buf, key = wbufs.next()
            view = shape_view(buf)
            P.dma(view, src_ap, [rkey] if rkey is not None else [], [key], "ld_" + key)
            return buf, key

        def norm_transpose(xg, xkey, col0, xT=None, xname="xnT"):
            xT = xnT if xT is None else xT
            for t in range(4):
                P.op("act", lambda e, t=t: e.activation(out=junk[:], in_=xg[:, t, :], func=AF.Square,
                                                        accum_out=ssq[:, col0 + t:col0 + t + 1]),
                     [xkey], ["junk", ("ssq", col0 + t)])
            P.op("act", lambda e: e.activation(out=rstd[:, col0:col0 + 4], in_=ssq[:, col0:col0 + 4], func=AF.Ln,
                                               scale=1.0 / D, bias=EPS),
                 [("ssq", col0 + t) for t in range(4)], [("rstd", col0)])
            P.op("act", lambda e: e.activation(out=rstd[:, col0:col0 + 4], in_=rstd[:, col0:col0 + 4], func=AF.Exp,
                                               scale=-0.5),
                 [("rstd", col0)], [("rstd", col0)])
            for t in range(4):
                P.op("dve", lambda e, t=t: e.tensor_scalar(out=xn[:, t, :], in0=xg[:, t, :],
                                                           scalar1=rstd[:, col0 + t:col0 + t + 1], scalar2=None,
                                                           op0=ALU.mult),
                     [xkey, ("rstd", col0)], [("xn", t)])
                bank, bkey = tr_banks.next()
                bv = bank[:].bitcast(BF16)
                for kc in range(8):
                    P.op("pe", lambda e, t=t, kc=kc, bv=bv: e.transpose(bv[:, kc * 128:(kc + 1) * 128],
                                                                        xn[:, t, kc * 128:(kc + 1) * 128], ident_b),
                         [("xn", t), "cstb"], [bkey])
                eng = ev.next()
                src = bv.rearrange("p (k c) -> p k c", k=8)
                dstv = xT[:, :, t * 128:(t + 1) * 128]
                if eng == "act":
                    P.op("act", lambda e, src=src, dstv=dstv: e.copy(out=dstv, in_=src), [bkey], [(xname, t)])
                else:
                    P.op("dve", lambda e, src=src, dstv=dstv: e.tensor_copy(out=dstv, in_=src), [bkey], [(xname, t)])

        xnT_keys = [("xnT", t) for t in range(4)]
        xnT2_keys = [("xnT2", t) for t in range(4)]

        def ffn(xg, xkey, wgn, wun, wdn):
            ffn_gu(wgn, wun)
            ffn_down(xg, xkey, wdn)

        def ffn_gu(wgn, wun):
            for mb in range(0, MC, 4):
                nm = min(4, MC - mb)
                gb, gkey = wload(wb[wgn][:, :, mb * 128:(mb + nm) * 128], lambda b: b[:, :, 0:nm * 128], ("wb", wgn))
                ub, ukey = wload(wb[wun][:, :, mb * 128:(mb + nm) * 128], lambda b: b[:, :, 0:nm * 128], ("wb", wun))
                for j in range(nm):
                    m = mb + j
                    pg, pgk = g_banks.next()
                    pu, puk = u_banks.next()
                    for kc in range(8):
                        P.op("pe", lambda e, kc=kc, j=j, pg=pg, gb=gb: e.matmul(
                            pg[:], lhsT=gb[:, kc, j * 128:(j + 1) * 128], rhs=xnT[:, kc, :],
                            start=(kc == 0), stop=(kc == 7)), [gkey] + xnT_keys, [pgk])
                    for kc in range(8):
                        P.op("pe", lambda e, kc=kc, j=j, pu=pu, ub=ub: e.matmul(
                            pu[:], lhsT=ub[:, kc, j * 128:(j + 1) * 128], rhs=xnT[:, kc, :],
                            start=(kc == 0), stop=(kc == 7)), [ukey] + xnT_keys, [puk])
                    sl, slk = sils.next()
                    P.op("act", lambda e, sl=sl, pg=pg: e.activation(out=sl[:], in_=pg[:], func=AF.Silu), [pgk], [slk])
                    P.op("dve", lambda e, sl=sl, pu=pu, m=m: e.tensor_tensor(out=hT[:, m, :], in0=sl[:], in1=pu[:],
                                                                             op=ALU.mult), [slk, puk], [("hT", m)])

        def ffn_down(xg, xkey, wdn):
            hkeys = [("hT", m) for m in range(MC)]
            for n in range(2):
                blocks = []
                for mb in range(0, MC, 8):
                    nm = min(8, MC - mb)
                    blocks.append((mb, nm) + wload(wb[wdn][:, mb:mb + nm, n * 512:(n + 1) * 512],
                                                  lambda b, nm=nm: b[:, 0:nm, :], ("wb", wdn)))
                for t in range(4):
                    pd, pdk = d_banks.next()
                    for (mb, nm, dbuf, dkey) in blocks:
                        for j in range(nm):
                            m = mb + j
                            P.op("pe", lambda e, m=m, j=j, t=t, pd=pd, dbuf=dbuf: e.matmul(
                                pd[:], lhsT=hT[:, m, t * 128:(t + 1) * 128], rhs=dbuf[:, j, :],
                                start=(m == 0), stop=(m == MC - 1)), [dkey] + hkeys, [pdk])
                    P.op("dve", lambda e, t=t, n=n, pd=pd: e.scalar_tensor_tensor(
                        out=xg[:, t, n * 512:(n + 1) * 512], in0=pd[:], scalar=0.5,
                        in1=xg[:, t, n * 512:(n + 1) * 512], op0=ALU.mult, op1=ALU.add), [pdk, xkey], [xkey])


        return SimpleNamespace(**{k: v for k, v in locals().items()})

    with ExitStack() as ph:
        C = ffn_ctx(ph, "a")
        xgs, norm_transpose, ffn, wload, fm_banks, d_banks = C.xgs, C.norm_transpose, C.ffn, C.wload, C.fm_banks, C.d_banks
        sqs, lns, qns, vst, zst, dst, ev, xnT, xnT_keys = C.sqs, C.lns, C.qns, C.vst, C.zst, C.dst, C.ev, C.xnT, C.xnT_keys
        ngroups = NTOK // GT
        xnT2a, xnT2a_keys = C.xnT2, C.xnT2_keys
        ffn_gu_a, ffn_down_a = C.ffn_gu, C.ffn_down
        pending = None
        lw32 = [ph.enter_context(nc.sbuf_tensor(f"sb_lw32_{i}", [128, 1024], F32)) for i in range(3)]
        lw16 = [ph.enter_context(nc.sbuf_tensor(f"sb_lw16_{i}", [128, 1024], BF16)) for i in range(3)]
        late = []
        for wname in ("wout", "w2g", "w2u", "w2d"):
            K, N = w_src[wname].shape
            fold = {"wout": V_GATT, "w2g": V_G2, "w2u": V_G2}.get(wname)
            for kc in range(K // 128):
                for c0 in range(0, N, 1024):
                    late.append((wname, kc, c0, min(N, c0 + 1024), fold))
        late_state = {"i": 0, "loaded": 0}

        def late_load(k):
            wname, kc, c0, c1, fold = late[k]
            sl = k % 3
            P.dma(lw32[sl][:, 0:c1 - c0], w_src[wname][kc * 128:(kc + 1) * 128, c0:c1], [], [f"lw32_{sl}"], f"ld_lw{sl}", q="act")

        def late_step(n):
            for _ in range(n):
                k = late_state["i"]
                if k >= len(late):
                    return
                while late_state["loaded"] < min(len(late), k + 2):
                    late_load(late_state["loaded"])
                    late_state["loaded"] += 1
                wname, kc, c0, c1, fold = late[k]
                sl = k % 3
                a32, a16, nn = lw32[sl], lw16[sl], c1 - c0
                if fold is None:
                    P.op("act", lambda e, a32=a32, a16=a16, nn=nn: e.copy(out=a16[:, 0:nn], in_=a32[:, 0:nn]),
                         [f"lw32_{sl}"], [f"lw16_{sl}"])
                else:
                    col = vecs[:, fold + kc:fold + kc + 1]
                    P.op("act", lambda e, a32=a32, a16=a16, nn=nn, col=col: e.activation(
                        out=a16[:, 0:nn], in_=a32[:, 0:nn], func=AF.Copy, scale=col),
                        [f"lw32_{sl}", "vecs"], [f"lw16_{sl}"])
                P.dma(wb[wname][:, kc, c0:c1], a16[:, 0:nn], [f"lw16_{sl}"], [("wb", wname)], f"st_lw{sl}", q="act")
                late_state["i"] += 1

        GX = {}

        def s_load(g):
            xg, xkey = xgs.next()
            GX[g] = (xg, xkey)
            tok0 = g * GT
            P.dma(xg[:], xin[tok0:tok0 + GT, :].rearrange("(t p) c -> p t c", p=128), [], [xkey], "ld_" + xkey)
            late_step(6)

        def s_rest(g):
            pending = None
            xg, xkey = GX[g]
            tok0 = g * GT
            P.dma(x1s[tok0:tok0 + GT, :].rearrange("(t p) c -> p t c", p=128), xg[:], [xkey], [("x1s", g)],
                  "st_" + xkey, q="pool")
            norm_transpose(xg, xkey, 4, xnT2a, "xnT2")
            sg = g - NPS // GT
            halo = (0 <= sg < HALO // GT) or (sg >= (HALO + TS) // GT)
            csl = slice(0, GT)
            if halo:
                if sg == HALO // GT - 1:
                    csl = slice(GT - 2, GT)
                elif sg == (HALO + TS) // GT:
                    csl = slice(0, 2)
                else:
                    csl = None
            for cb in range(8):
                if halo and (cb < 2 or (cb >= 4 and csl is None)):
                    continue
                col0 = cb * 512 if cb < 4 else 4096 + (cb - 4) * 512
                wbf, wkey = wload(wb["win"][:, :, col0:col0 + 512], lambda b: b[:, :, :], ("wb", "win"))
                for j in range(4):
                    ch = cb * 4 + j
                    pb, pbk = fm_banks.next()
                    msl = csl if ch >= 16 else slice(0, GT)
                    for kc in range(8):
                        P.op("pe", lambda e, kc=kc, j=j, pb=pb, wbf=wbf, msl=msl: e.matmul(
                            pb[:, msl], lhsT=wbf[:, kc, j * 128:(j + 1) * 128], rhs=xnT2a[:, kc, msl],
                            start=(kc == 0), stop=(kc == 7)), [wkey] + xnT2a_keys, [pbk])
                    def post(ch=ch, pb=pb, pbk=pbk, msl=msl):
                        if ch < 16:
                            sq, sqk = sqs.next()
                            ln, lnk = lns.next()
                            qn, qnk = qns.next()
                            ps, psk = d_banks.next()
                            gcol = vecs[:, V_GQ:V_GQ + 1] if ch < 8 else vecs[:, V_GK:V_GK + 1]
                            P.op("act", lambda e, sq=sq, pb=pb: e.activation(out=sq[:], in_=pb[:], func=AF.Square), [pbk], [sqk])
                            P.op("pe", lambda e, sq=sq, ps=ps: e.matmul(ps[:], lhsT=blk_b, rhs=sq[:], start=True, stop=True),
                                 [sqk, "cstb"], [psk])
                            P.op("act", lambda e, ln=ln, ps=ps: e.activation(out=ln[:], in_=ps[:], func=AF.Ln,
                                                                             scale=1.0 / 64, bias=EPS), [psk], [lnk])
                            P.op("act", lambda e, ln=ln: e.activation(out=ln[:], in_=ln[:], func=AF.Exp, scale=-0.5),
                                 [lnk], [lnk])
                            P.op("dve", lambda e, qn=qn, pb=pb, ln=ln, gcol=gcol: e.scalar_tensor_tensor(
                                out=qn[:], in0=pb[:], scalar=gcol, in1=ln[:], op0=ALU.mult, op1=ALU.mult),
                                [pbk, lnk, "vecs"], [qnk])
                            dstT = qT if ch < 8 else kT
                            r0 = (ch % 8) * 128
                            P.dma(dstT[r0:r0 + 128, tok0:tok0 + GT], qn[:], [qnk], [("qk", ch, g)], "st_" + qnk, q="pool")
                        else:
                            qn, qnk = qns.next()
                            eng = ev.next()
                            if eng == "act":
                                P.op("act", lambda e, qn=qn, pb=pb: e.copy(out=qn[:, msl], in_=pb[:, msl]), [pbk], [qnk])
                            else:
                                P.op("dve", lambda e, qn=qn, pb=pb: e.tensor_copy(out=qn[:, msl], in_=pb[:, msl]), [pbk], [qnk])
                            r0 = (ch - 16) * 128
                            P.dma(xbcT[r0:r0 + 128, tok0 + msl.start:tok0 + msl.stop], qn[:, msl], [qnk], [("xbc", ch, g)],
                                  "st_" + qnk, q="pool")

                    if pending is not None:
                        pending()
                    pending = post
            if pending is not None:
                pending()
                pending = None
            for blk in range(5):
                if halo and blk >= 2:
                    continue
                if blk < 4:
                    col0 = 2048 + blk * 512
                    ncol = 512
                else:
                    col0 = 6144
                    ncol = 32
                wbf, wkey = wload(wb["win"][:, :, col0:col0 + ncol], lambda b, ncol=ncol: b[:, :, 0:ncol], ("wb", "win"))
                for t in range(4):
                    pb, pbk = fm_banks.next()
                    for kc in range(8):
                        P.op("pe", lambda e, kc=kc, t=t, pb=pb, wbf=wbf, ncol=ncol: e.matmul(
                            pb[:, 0:ncol], lhsT=xnT2a[:, kc, t * 128:(t + 1) * 128], rhs=wbf[:, kc, 0:ncol],
                            start=(kc == 0), stop=(kc == 7)), [wkey] + xnT2a_keys, [pbk])
                    if blk < 2:
                        dsti, dkey = vst[:, t, blk * 8:(blk + 1) * 8, 0:64], "vst"
                    elif blk < 4:
                        dsti, dkey = zst[:, t, (blk - 2) * 512:(blk - 1) * 512], "zst"
                    else:
                        dsti, dkey = dst[:, t, :], "dst"
                    eng = ev.next()
                    srcv = pb[:, 0:ncol].rearrange("p (h d) -> p h d", d=64) if blk < 2 else pb[:, 0:ncol]
                    if eng == "act":
                        P.op("act", lambda e, dsti=dsti, srcv=srcv: e.copy(out=dsti, in_=srcv), [pbk], [dkey])
                    else:
                        P.op("dve", lambda e, dsti=dsti, srcv=srcv: e.tensor_copy(out=dsti, in_=srcv), [pbk], [dkey])
            P.dma(vtok[tok0:tok0 + GT, :].rearrange("(t p) c -> p t c", p=128), vst[:].rearrange("p t h d -> p t (h d)"), ["vst"], [("vtok", g)], "st_vst", q="pool")
            if not halo:
                P.dma(ztok[tok0:tok0 + GT, :].rearrange("(t p) c -> p t c", p=128), zst[:], ["zst"], [("ztok", g)], "st_zst", q="pool")
                P.dma(dtr[tok0:tok0 + GT, :].rearrange("(t p) c -> p t c", p=128), dst[:], ["dst"], [("dtr", g)], "st_dst", q="pool")

        s_load(0)
        norm_transpose(GX[0][0], GX[0][1], 0)
        ffn_gu_a("w1g", "w1u")
        ew32 = [ph.enter_context(nc.sbuf_tensor(f"sb_ew32_{i}", [128, 1024], F32)) for i in range(3)]
        ew16 = [ph.enter_context(nc.sbuf_tensor(f"sb_ew16_{i}", [128, 1024], BF16)) for i in range(3)]
        ek = 0
        for wname in ("w1d", "win"):
            Kw, Nw = w_src[wname].shape
            foldw = V_GMIX if wname == "win" else None
            for kc in range(Kw // 128):
                for c0 in range(0, Nw, 1024):
                    c1 = min(Nw, c0 + 1024)
                    sl = ek % 3
                    ek += 1
                    a32, a16, nn = ew32[sl], ew16[sl], c1 - c0
                    P.dma(a32[:, 0:nn], w_src[wname][kc * 128:(kc + 1) * 128, c0:c1], [], [f"ew32_{sl}"], f"ld_ew{sl}")
                    eng = "dve" if sl % 2 == 0 else "act"
                    if foldw is None:
                        if eng == "act":
                            P.op("act", lambda e, a32=a32, a16=a16, nn=nn: e.copy(out=a16[:, 0:nn], in_=a32[:, 0:nn]),
                                 [f"ew32_{sl}"], [f"ew16_{sl}"])
                        else:
                            P.op("dve", lambda e, a32=a32, a16=a16, nn=nn: e.tensor_copy(out=a16[:, 0:nn], in_=a32[:, 0:nn]),
                                 [f"ew32_{sl}"], [f"ew16_{sl}"])
                    else:
                        col = vecs[:, foldw + kc:foldw + kc + 1]
                        if eng == "act":
                            P.op("act", lambda e, a32=a32, a16=a16, nn=nn, col=col: e.activation(
                                out=a16[:, 0:nn], in_=a32[:, 0:nn], func=AF.Copy, scale=col), [f"ew32_{sl}", "vecs"], [f"ew16_{sl}"])
                        else:
                            P.op("dve", lambda e, a32=a32, a16=a16, nn=nn, col=col: e.tensor_scalar(
                                out=a16[:, 0:nn], in0=a32[:, 0:nn], scalar1=col, scalar2=None, op0=ALU.mult),
                                [f"ew32_{sl}", "vecs"], [f"ew16_{sl}"])
                    P.dma(wb[wname][:, kc, c0:c1], a16[:, 0:nn], [f"ew16_{sl}"], [("wb", wname)], f"st_ew{sl}", q="pool")
        ffn_down_a(GX[0][0], GX[0][1], "w1d")
        for g in range(ngroups):
            if g + 1 < ngroups:
                s_load(g + 1)
                norm_transpose(GX[g + 1][0], GX[g + 1][1], 0)
                ffn_gu_a("w1g", "w1u")
            s_rest(g)
            if g + 1 < ngroups:
                ffn_down_a(GX[g + 1][0], GX[g + 1][1], "w1d")
        late_step(len(late))
        P.barrier()
    if stop_after == "A":
        return finish(nc, es, P, sems, None)

    xs_tok = dscr("xs_tok", [NTOK, D], BF16)
    B_tok = dscr("B_tok", [NTOK, 512], BF16)
    BTs = dscr("BTs", [512, NTOK], BF16)
    CTs = dscr("CTs", [512, NTOK], BF16)
    yf_d = dscr("yf_d", [NTOK, D], F32)
    ssm_tok = dscr("ssm_tok", [NTOK, D], BF16)
    xbcT_v = xbcT.rearrange("(c p) t -> p c t", p=128)
    BTs_v = BTs.rearrange("(g p) t -> p g t", p=128)
    CTs_v = CTs.rearrange("(g p) t -> p g t", p=128)
    S0 = NPS + HALO
    own_groups = [(g * GT, "P", g, 4) for g in range(NPS // GT)] + [(S0 + g * GT, "S", g, TS // GT) for g in range(TS // GT)]
    with ExitStack() as ph:
        def psb(name, shape, dt):
            return ph.enter_context(nc.sbuf_tensor("sb_" + name, list(shape), dt))
        xcs = Rot([(psb(f"xc{i}", [128, 16, GT + 4], BF16), f"xc{i}") for i in range(2)])
        cvTs = Rot([(psb(f"cvT{i}", [128, 16, GT], BF16), f"cvT{i}") for i in range(2)])
        dg = psb("dg", [128, 5, 16, 128], BF16)
        for tap in range(5):
            for ch in range(16):
                eng = "dve" if (tap * 16 + ch) % 2 == 0 else "pool"
                wc = vecs[:, V_CW + tap * 16 + ch:V_CW + tap * 16 + ch + 1]
                P.op(eng, lambda e, tap=tap, ch=ch, wc=wc: e.tensor_scalar(out=dg[:, tap, ch, :], in0=ident_b, scalar1=wc, scalar2=None,
                                                                           op0=ALU.mult), ["cstb", "vecs"], [("dg", tap, ch)])
        xs_st = psb("xs_st", [128, 4, D], BF16)
        B_st = psb("B_st", [128, 4, 512], BF16)
        cv_banks = Rot([bk(4), bk(5), bk(6), bk(7)])
        tr_banks = Rot([bk(0), bk(1), bk(2), bk(3)])
        ev = Rot(["dve", "act"])
        for (tok0, seg, gi, ng) in own_groups:
            xc, xck = xcs.next()
            cvT, cvk = cvTs.next()
            lo, hi = tok0 - 2, tok0 + GT + 2
            c0, c1 = 0, GT + 4
            if seg == "P" and gi == 0:
                P.op("pool", lambda e, xc=xc: e.memset(xc[:, :, 0:2], 0.0), [], [xck])
                lo, c0 = tok0, 2
            if seg == "P" and gi == ng - 1:
                P.op("pool", lambda e, xc=xc: e.memset(xc[:, :, GT + 2:GT + 4], 0.0), [], [xck])
                hi, c1 = tok0 + GT, GT + 2
            P.dma(xc[:, :, c0:c1], xbcT_v[:, :, lo:hi], [("xbc", 16 + ch, tok0 // GT) for ch in range(16)], [xck], "ld_" + xck)
            for ch in range(16):
                pb, pbk = cv_banks.next()
                for tap in range(5):
                    P.op("pe", lambda e, tap=tap, ch=ch, pb=pb, xc=xc: e.matmul(
                        pb[:], lhsT=dg[:, tap, ch, :], rhs=xc[:, ch, tap:tap + GT], start=(tap == 0), stop=(tap == 4)),
                        [xck, ("dg", tap, ch)], [pbk])
                bcol = vecs[:, V_CB + ch:V_CB + ch + 1]
                P.op("act", lambda e, pb=pb, ch=ch, bcol=bcol, cvT=cvT: e.activation(out=cvT[:, ch, :], in_=pb[:], func=AF.Silu, bias=bcol),
                     [pbk, "vecs"], [(cvk, ch)])
            P.dma(BTs_v[:, :, tok0:tok0 + GT], cvT[:, 8:12, :], [(cvk, ch) for ch in range(8, 12)], [("BTs", tok0)], "st_bt", q="pool")
            P.dma(CTs_v[:, :, tok0:tok0 + GT], cvT[:, 12:16, :], [(cvk, ch) for ch in range(12, 16)], [("CTs", tok0)], "st_ct", q="pool")
            for t in range(4):
                bank, bkey = tr_banks.next()
                bv = bank[:].bitcast(BF16)
                for ch in range(8):
                    P.op("pe", lambda e, t=t, ch=ch, bv=bv, cvT=cvT: e.transpose(bv[:, ch * 128:(ch + 1) * 128],
                                                                        cvT[:, ch, t * 128:(t + 1) * 128], ident_b),
                         [(cvk, ch), "cstb"], [bkey])
                eng = ev.next()
                if eng == "act":
                    P.op("act", lambda e, t=t, bv=bv: e.copy(out=xs_st[:, t, :], in_=bv), [bkey], ["xs_st"])
                else:
                    P.op("dve", lambda e, t=t, bv=bv: e.tensor_copy(out=xs_st[:, t, :], in_=bv), [bkey], ["xs_st"])
                bank, bkey = tr_banks.next()
                bv = bank[:].bitcast(BF16)
                for ch in range(8, 12):
                    P.op("pe", lambda e, t=t, ch=ch, bv=bv, cvT=cvT: e.transpose(bv[:, (ch - 8) * 128:(ch - 7) * 128],
                                                                        cvT[:, ch, t * 128:(t + 1) * 128], ident_b),
                         [(cvk, ch), "cstb"], [bkey])
                eng = ev.next()
                if eng == "act":
                    P.op("act", lambda e, t=t, bv=bv: e.copy(out=B_st[:, t, :], in_=bv[:, 0:512]), [bkey], ["B_st"])
                else:
                    P.op("dve", lambda e, t=t, bv=bv: e.tensor_copy(out=B_st[:, t, :], in_=bv[:, 0:512]), [bkey], ["B_st"])
            P.dma(xs_tok[tok0:tok0 + GT, :].rearrange("(t p) c -> p t c", p=128), xs_st[:], ["xs_st"], [("xs_tok", tok0)], "st_xs", q="pool")
            P.dma(B_tok[tok0:tok0 + GT, :].rearrange("(t p) c -> p t c", p=128), B_st[:], ["B_st"], [("B_tok", tok0)], "st_bs", q="pool")
        P.barrier()
    if stop_after == "B1":
        return finish(nc, es, P, sems, None)

    sel_d = din("sel", [128, 4])
    ccin = [nc.dram_tensor(f"ccin{i}", [128, D if i < 2 else 32], F32).ap() for i in range(3)]
    ccout = [nc.dram_tensor(f"ccout{i}", [512, D if i < 2 else 32], F32).ap() for i in range(3)]
    with ExitStack() as ph:
        def psb(name, shape, dt):
            return ph.enter_context(nc.sbuf_tensor("sb_" + name, list(shape), dt))
        ident_f = cst[:, C_ID:C_ID + 128]
        ones_f = cst[:, C_ONES:C_ONES + 128]
        tri = [cst[:, C_TRIF:C_TRIF + 128], cst[:, C_TRIB:C_TRIB + 128]]
        A_t = psb("A_t", [128, 32], F32)
        P.op("act", lambda e: e.activation(out=A_t[:], in_=vecs[:, V_ALOG:V_ALOG + 32], func=AF.Exp), ["vecs"], ["A_t"])
        P.op("dve", lambda e: e.tensor_scalar(out=A_t[:], in0=A_t[:], scalar1=-1.0, scalar2=None, op0=ALU.mult), ["A_t"], ["A_t"])
        sel = psb("sel", [128, 4], F32)
        P.dma(sel[:], sel_d, [], ["sel"], "ld_sel")
        xsg = Rot([(psb(f"xsg{i}", [128, 4, D], BF16), f"xsg{i}") for i in range(2)])
        Bg = Rot([(psb(f"Bg{i}", [128, 4, 512], BF16), f"Bg{i}") for i in range(2)])
        BTg = Rot([(psb(f"BTg{i}", [128, 4, GT], BF16), f"BTg{i}") for i in range(2)])
        CTg = Rot([(psb(f"CTg{i}", [128, 4, GT], BF16), f"CTg{i}") for i in range(2)])
        dtg = Rot([(psb(f"dtg{i}", [128, 4, 32], F32), f"dtg{i}") for i in range(2)])
        yfg = Rot([(psb(f"yfg{i}", [128, 4, D], F32), f"yfg{i}") for i in range(2)])
        zg = Rot([(psb(f"zg{i}", [128, 4, D], BF16), f"zg{i}") for i in range(2)])
        ssm_st = psb("ssm_st", [128, 4, D], BF16)
        pres = Rot([(psb(f"pre{i}", [128, 48], F32), f"pre{i}") for i in range(2)])
        exs = Rot([(psb(f"ex{i}", [128, 48], F32), f"ex{i}") for i in range(2)])
        nacs = Rot([(psb(f"nac{i}", [128, 16], F32), f"nac{i}") for i in range(2)])
        dts = Rot([(psb(f"dt{i}", [128, 16], F32), f"dt{i}") for i in range(2)])
        avs = Rot([(psb(f"av{i}", [128, 16], F32), f"av{i}") for i in range(2)])
        dtxs = Rot([(psb(f"dtx{i}", [128, D], BF16), f"dtx{i}") for i in range(2)])
        wdtxs = Rot([(psb(f"wdtx{i}", [128, D], BF16), f"wdtx{i}") for i in range(2)])
        cbms = Rot([(psb(f"cbm{i}", [128, 4, 128], BF16), f"cbm{i}") for i in range(2)])
        segs = Rot([(psb(f"seg{i}", [128, 4, 128], F32), f"seg{i}") for i in range(2)])
        lexs = Rot([(psb(f"lex{i}", [128, 4, 128], BF16), f"lex{i}") for i in range(2)])
        Ms = Rot([(psb(f"M{i}", [128, 4, 128], BF16), f"M{i}") for i in range(3)])
        tmps = Rot([(psb(f"tmp{i}", [128, 512], F32), f"tmp{i}") for i in range(2)])
        sig = psb("sig", [128, D], F32)
        yaccs = Rot([(psb(f"yacc{i}", [128, D], F32), f"yacc{i}") for i in range(2)])
        gss = psb("gss", [128, 8], F32)
        junk2 = psb("junk2", [128, 256], F32)
        H = [psb(f"H{d}", [128, D], F32) for d in range(2)]
        Hbf = [psb(f"Hbf{d}", [128, D], BF16) for d in range(2)]
        Hin = [psb(f"Hin{d}", [128, D], F32) for d in range(2)]
        Dacc = psb("Dacc", [128, 32], F32)
        bA, bC = bk(0), bk(1)
        bL = Rot([bk(2), bk(3)])
        bY = Rot([bk(4), bk(5)])
        bO = bk(6)
        bS = Rot([bk(7)])

        def h3(ap):
            return ap.rearrange("p (h d) -> p h d", d=64)

        def bc3(ap16, n=64):
            return ap16.unsqueeze(2).to_broadcast([128, ap16.shape[1], n])

        def load_group(tok0, full, need_y):
            a, ak = xsg.next(); b, bkk = Bg.next(); dtt, dtk = dtg.next()
            rr = lambda ap: ap[tok0:tok0 + GT, :].rearrange("(t p) c -> p t c", p=128)
            P.dma(a[:], rr(xs_tok), [("xs_tok", tok0)], [ak], "ld_" + ak)
            P.dma(b[:], rr(B_tok), [("B_tok", tok0)], [bkk], "ld_" + bkk)
            P.dma(dtt[:], rr(dtr), [("dtr", tok0 // GT)], [dtk], "ld_" + dtk)
            res = dict(xs=(a, ak), B=(b, bkk), dt=(dtt, dtk))
            if full:
                bt, btk = BTg.next(); ct, ctk = CTg.next()
                P.dma(bt[:], BTs_v[:, :, tok0:tok0 + GT], [("BTs", tok0)], [btk], "ld_" + btk)
                P.dma(ct[:], CTs_v[:, :, tok0:tok0 + GT], [("CTs", tok0)], [ctk], "ld_" + ctk)
                res["BT"] = (bt, btk); res["CT"] = (ct, ctk)
            if need_y:
                yy, yk = yfg.next(); zz, zk = zg.next()
                P.dma(yy[:], rr(yf_d), [("yf", tok0)], [yk], "ld_" + yk)
                P.dma(zz[:], rr(ztok), [("ztok", tok0 // GT)], [zk], "ld_" + zk)
                res["yf"] = (yy, yk); res["z"] = (zz, zk)
            return res

        def ssd_chunk(G, t, d, full):
            xs_t, xsk = G["xs"]; B_t, Bk = G["B"]; dt_t, dtk = G["dt"]
            pre, prek = pres.next(); ex, exk = exs.next(); nac, nack = nacs.next()
            dtv, dtvk = dts.next(); av, avk = avs.next()
            dtx, dtxk = dtxs.next(); wdtx, wdtxk = wdtxs.next()
            Hd, Hk, Hb, Hbk = H[d], f"H{d}", Hbf[d], f"Hbf{d}"
            P.op("dve", lambda e: e.tensor_tensor(out=dtv[:], in0=dt_t[:, t, d * 16:(d + 1) * 16],
                                                  in1=vecs[:, V_DTB + d * 16:V_DTB + (d + 1) * 16], op=ALU.add),
                 [dtk, "vecs"], [dtvk])
            P.op("act", lambda e: e.activation(out=dtv[:], in_=dtv[:], func=AF.Exp), [dtvk], [dtvk])
            P.op("act", lambda e: e.activation(out=dtv[:], in_=dtv[:], func=AF.Ln, bias=1.0), [dtvk], [dtvk])
            P.op("dve", lambda e: e.tensor_tensor(out=av[:], in0=dtv[:], in1=A_t[:, d * 16:(d + 1) * 16], op=ALU.mult),
                 [dtvk, "A_t"], [avk])
            pa, pak = bA
            P.op("pe", lambda e: e.matmul(pa[:, 0:16], lhsT=tri[d], rhs=av[:], start=True, stop=True), [avk, "cst"], [pak])
            P.op("pe", lambda e: e.matmul(pa[:, 16:32], lhsT=ones_f, rhs=av[:], start=True, stop=True), [avk, "cst"], [pak])
            P.op("dve", lambda e: e.tensor_copy(out=pre[:, 0:32], in_=pa[:, 0:32]), [pak], [prek])
            P.op("dve", lambda e: e.tensor_tensor(out=pre[:, 32:48], in0=pre[:, 16:32], in1=pre[:, 0:16], op=ALU.subtract),
                 [prek], [prek])
            P.op("act", lambda e: e.activation(out=ex[:], in_=pre[:], func=AF.Exp), [prek], [exk])
            P.op("pool" if full else "dve", lambda e: e.tensor_tensor(out=h3(dtx[:]), in0=h3(xs_t[:, t, :]), in1=bc3(dtv[:, 0:16]), op=ALU.mult),
                 [xsk, dtvk], [dtxk])
            P.op("pool", lambda e: e.tensor_tensor(out=h3(wdtx[:]), in0=h3(dtx[:]), in1=bc3(ex[:, 32:48]), op=ALU.mult),
                 [dtxk, exk], [wdtxk])
            if full:
                BT_t, BTk = G["BT"]; CT_t, CTk = G["CT"]
                P.op("dve", lambda e: e.tensor_scalar(out=nac[:], in0=pre[:, 0:16], scalar1=-1.0, scalar2=None, op0=ALU.mult),
                     [prek], [nack])
                pc, pck = bC
                for g in range(4):
                    P.op("pe", lambda e, g=g: e.matmul(pc[:, g * 128:(g + 1) * 128], lhsT=BT_t[:, g, t * 128:(t + 1) * 128],
                                                       rhs=CT_t[:, g, t * 128:(t + 1) * 128], start=True, stop=True),
                         [BTk, CTk], [pck])
                cbm, cbmk = cbms.next()
                P.op("dve", lambda e: e.tensor_tensor(out=cbm[:], in0=pc[:].rearrange("p (g l) -> p g l", g=4),
                                                      in1=tri[d].unsqueeze(1).to_broadcast([128, 4, 128]), op=ALU.mult),
                     [pck, "cst"], [cbmk])

            def main(ystage, hook=None):
                if full:
                    BT_t, BTk = G["BT"]; CT_t, CTk = G["CT"]
                    main_full(ystage, BT_t, BTk, CT_t, CTk, hook)
                return main_tail()

            def main_full(ystage, BT_t, BTk, CT_t, CTk, hook=None):
                    pl_of = {}

                    def emit_T(g):
                        pl, plk = bL.next()
                        for j in range(4):
                            h = g * 4 + j
                            P.op("pe", lambda e, j=j, h=h, pl=pl: e.transpose(pl[:, j * 128:(j + 1) * 128],
                                                                              pre[:, h:h + 1].to_broadcast([128, 128]), ident_f),
                                 [prek, "cst"], [plk])
                        pl_of[g] = (pl, plk)

                    emit_T(0)
                    for hh in range(2):
                        py, pyk = bY.next()
                        po, pok = bO
                        for gg in range(2):
                            g = hh * 2 + gg
                            if g + 1 < 4:
                                emit_T(g + 1)
                            pl, plk = pl_of[g]
                            lx, lxk = lexs.next(); M, Mk = Ms.next()
                            for j in range(4):
                                h = g * 4 + j
                                P.op("act", lambda e, j=j, h=h, pl=pl, lx=lx: e.activation(
                                    out=lx[:, j, :], in_=pl[:, j * 128:(j + 1) * 128], func=AF.Exp, bias=nac[:, h:h + 1]),
                                    [plk, nack], [lxk])
                            P.op("dve", lambda e, lx=lx, M=M, g=g: e.scalar_tensor_tensor(
                                out=M[:], in0=lx[:], scalar=1.0, in1=cbm[:, g:g + 1, :].to_broadcast([128, 4, 128]),
                                op0=ALU.min, op1=ALU.mult), [lxk, cbmk], [Mk])
                            for j in range(4):
                                h = g * 4 + j
                                c0 = (h % 8) * 64
                                P.op("pe", lambda e, j=j, h=h, c0=c0, M=M, py=py: e.matmul(
                                    py[:, c0:c0 + 64], lhsT=M[:, j, :], rhs=dtx[:, h * 64:(h + 1) * 64], start=True, stop=True),
                                    [Mk, dtxk], [pyk])
                            P.op("pe", lambda e, g=g, gg=gg, po=po: e.matmul(
                                po[:, gg * 256:(gg + 1) * 256], lhsT=CT_t[:, g, t * 128:(t + 1) * 128],
                                rhs=Hb[:, g * 256:(g + 1) * 256], start=True, stop=True), [CTk, Hbk], [pok])
                            if hook is not None:
                                hook()
                        tm, tmk = tmps.next()
                        P.op("dve", lambda e, hh=hh, po=po, tm=tm: e.tensor_tensor(
                            out=tm[:].rearrange("p (h d) -> p h d", d=64), in0=po[:].rearrange("p (h d) -> p h d", d=64),
                            in1=bc3(ex[:, hh * 8:(hh + 1) * 8]), op=ALU.mult), [pok, exk], [tmk])
                        P.op("dve", lambda e, hh=hh, py=py, tm=tm: e.tensor_tensor(
                            out=ystage[0][:, hh * 512:(hh + 1) * 512], in0=tm[:], in1=py[:], op=ALU.add), [pyk, tmk], [ystage[1]])
                        if hook is not None:
                            hook()

            def main_tail():
                for hh in range(2):
                    ps_, psk = bS.next()
                    for gg in range(2):
                        g = hh * 2 + gg
                        P.op("pe", lambda e, g=g, gg=gg, ps_=ps_: e.matmul(
                            ps_[:, gg * 256:(gg + 1) * 256], lhsT=B_t[:, t, g * 128:(g + 1) * 128],
                            rhs=wdtx[:, g * 256:(g + 1) * 256], start=True, stop=True), [Bk, wdtxk], [psk])
                    sl = slice(hh * 512, (hh + 1) * 512)
                    P.op("pool" if full else "dve", lambda e, hh=hh, sl=sl: e.tensor_tensor(
                        out=Hd[:, sl].rearrange("p (h d) -> p h d", d=64), in0=Hd[:, sl].rearrange("p (h d) -> p h d", d=64),
                        in1=bc3(ex[:, 16 + hh * 8:16 + (hh + 1) * 8]), op=ALU.mult), [Hk, exk], [Hk])
                    P.op("dve", lambda e, sl=sl, ps_=ps_: e.tensor_tensor(out=Hd[:, sl], in0=Hd[:, sl], in1=ps_[:], op=ALU.add),
                         [Hk, psk], [Hk])
                if full:
                    P.op("act", lambda e: e.copy(out=Hb[:], in_=Hd[:]), [Hk], [Hbk])
                return ex, exk

            return main

        def init_state(d, src=None):
            if src is None:
                P.op("pool", lambda e: e.memset(H[d][:], 0.0), [], [f"H{d}"])
            else:
                P.op("pool", lambda e: e.tensor_copy(out=H[d][:], in_=src[0][:]), [src[1]], [f"H{d}"])
            P.op("act", lambda e: e.copy(out=Hbf[d][:], in_=H[d][:]), [f"H{d}"], [f"Hbf{d}"])

        def sweep(base, nchunks, d, full, light_acc=False):
            groups = list(range(nchunks // 4))
            if d == 1:
                groups = groups[::-1]
            ts = [0, 1, 2, 3] if d == 0 else [3, 2, 1, 0]
            chunks = [(gi, t) for gi in groups for t in ts]
            Gs = {}

            def get_group(gi):
                if gi not in Gs:
                    Gs[gi] = load_group(base + gi * GT, full, need_y=(full and d == 1))
                return Gs[gi]

            def do_pre(i):
                gi, t = chunks[i]
                return ssd_chunk(get_group(gi), t, d, full)

            deferred = []

            def run_deferred():
                if deferred:
                    deferred.pop(0)()

            nxt = do_pre(0)
            for i, (gi, t) in enumerate(chunks):
                main = nxt
                G = get_group(gi)
                tok0 = base + gi * GT
                if i + 1 < len(chunks):
                    nxt = do_pre(i + 1)
                if full and d == 0:
                    if "ystage" not in G:
                        G["ystage"] = yfg.next()
                    yy, yk = G["ystage"]
                    ex, exk = main((yy[:, t, :], yk))
                elif full:
                    yacc, yak = yaccs.next()
                    ex, exk = main((yacc[:], yak), hook=run_deferred)
                    yy, yk = G["yf"]; zz, zk = G["z"]; xs_t, xsk = G["xs"]

                    def part1(yacc=yacc, yak=yak, yy=yy, yk=yk, xs_t=xs_t, xsk=xsk, t=t):
                        P.op("dve", lambda e: e.tensor_tensor(out=yacc[:], in0=yacc[:], in1=yy[:, t, :], op=ALU.add), [yak, yk], [yak])
                        P.op("pool", lambda e: e.tensor_tensor(out=h3(sig[:]), in0=h3(xs_t[:, t, :]), in1=bc3(vecs[:, V_DSK:V_DSK + 16]),
                                                               op=ALU.mult), [xsk, "vecs"], ["sig"])

                    def part2(yacc=yacc, yak=yak, zz=zz, zk=zk, t=t):
                        P.op("dve", lambda e: e.tensor_tensor(out=yacc[:], in0=yacc[:], in1=sig[:], op=ALU.add), [yak, "sig"], [yak])
                        P.op("act", lambda e: e.activation(out=sig[:], in_=zz[:, t, :], func=AF.Silu), [zk, "sig"], ["sig"])

                    def part3(yacc=yacc, yak=yak):
                        P.op("dve", lambda e: e.tensor_tensor(out=yacc[:], in0=yacc[:], in1=sig[:], op=ALU.mult), [yak, "sig"], [yak])
                        for g in range(4):
                            P.op("act", lambda e, g=g: e.activation(out=junk2[:], in_=yacc[:, g * 256:(g + 1) * 256], func=AF.Square,
                                                                    accum_out=gss[:, g:g + 1]), [yak], ["junk2", ("gss", g)])

                    def part4(yacc=yacc, yak=yak, t=t):
                        P.op("act", lambda e: e.activation(out=gss[:, 4:8], in_=gss[:, 0:4], func=AF.Ln, scale=1.0 / 256, bias=EPS),
                             [("gss", g) for g in range(4)], ["gssr"])
                        P.op("act", lambda e: e.activation(out=gss[:, 4:8], in_=gss[:, 4:8], func=AF.Exp, scale=-0.5), ["gssr"], ["gssr"])
                        for g in range(4):
                            P.op("dve", lambda e, g=g: e.tensor_scalar(
                                out=ssm_st[:, t, g * 256:(g + 1) * 256], in0=yacc[:, g * 256:(g + 1) * 256],
                                scalar1=gss[:, 4 + g:5 + g], scalar2=None, op0=ALU.mult), [yak, "gssr"], ["ssm_st"])

                    deferred.extend([part1, part2, part3, part4])
                    if t == ts[-1]:
                        def store_part(tok0=tok0):
                            P.dma(ssm_tok[tok0:tok0 + GT, :].rearrange("(t p) c -> p t c", p=128), ssm_st[:], ["ssm_st"],
                                  [("ssm_tok", tok0)], "st_ssm", q="pool")
                        deferred.append(store_part)
                else:
                    ex, exk = main(None)
                if light_acc:
                    P.op("dve", lambda e, ex=ex: e.tensor_tensor(out=Dacc[:, d * 16:(d + 1) * 16], in0=Dacc[:, d * 16:(d + 1) * 16],
                                                                 in1=ex[:, 16:32], op=ALU.mult), [exk, "Dacc"], ["Dacc"])
                last_in_group = (t == ts[-1])
                if last_in_group and full and d == 0:
                    yy, yk = G["ystage"]
                    P.dma(yf_d[tok0:tok0 + GT, :].rearrange("(t p) c -> p t c", p=128), yy[:], [yk], [("yf", tok0)], "st_" + yk, q="pool")

            while deferred:
                deferred.pop(0)()

        P.op("pool", lambda e: e.memset(Dacc[:], 1.0), [], ["Dacc"])
        for d in range(2):
            init_state(d)
            sweep(S0, TS // 128, d, False, light_acc=True)
            P.dma(ccin[d], H[d][:], [f"H{d}"], [f"ccin{d}"], f"st_cc{d}", q="pool")
        P.dma(ccin[2], Dacc[:], ["Dacc"], ["ccin2"], "st_cc2", q="pool")
        for i in range(3):
            P.op("pool", lambda e, i=i: e.collective_compute("AllGather", ALU.bypass, replica_groups=[[0, 1, 2, 3], [4, 5, 6, 7]],
                                                             ins=[ccin[i].opt()], outs=[ccout[i].opt()]),
                 [f"ccin{i}"], [f"ccout{i}"], chan=f"cc{i}", inc=1)
        for d in range(2):
            init_state(d)
            sweep(0, NPS // 128, d, True)
        Ff = Rot([(psb(f"Ff{i}", [128, D], F32), f"Ff{i}") for i in range(2)])
        Dd = Rot([(psb(f"Dd{i}", [128, 32], F32), f"Dd{i}") for i in range(2)])
        Gc = psb("Gc", [128, D], F32)
        for d in range(2):
            P.op("pool", lambda e: e.memset(Gc[:], 0.0), [], ["Gc"])
            P.op("pool", lambda e, d=d: e.memset(Hin[d][:], 0.0), [], [f"Hin{d}"])
            order = [0, 1, 2, 3] if d == 0 else [3, 2, 1, 0]
            for i in order:
                ff, ffk = Ff.next(); dd, ddk = Dd.next()
                P.dma(ff[:], ccout[d][i * 128:(i + 1) * 128, :], [f"ccout{d}"], [ffk], "ld_" + ffk)
                P.dma(dd[:], ccout[2][i * 128:(i + 1) * 128, :], ["ccout2"], [ddk], "ld_" + ddk)
                P.op("dve", lambda e, d=d, i=i: e.scalar_tensor_tensor(out=Hin[d][:], in0=Gc[:], scalar=sel[:, i:i + 1], in1=Hin[d][:],
                                                                       op0=ALU.mult, op1=ALU.add), ["Gc", "sel", f"Hin{d}"], [f"Hin{d}"])
                P.op("dve", lambda e, d=d, dd=dd: e.tensor_tensor(out=h3(Gc[:]), in0=h3(Gc[:]), in1=bc3(dd[:, d * 16:(d + 1) * 16]),
                                                                  op=ALU.mult), ["Gc", ddk], ["Gc"])
                P.op("dve", lambda e, ff=ff: e.tensor_tensor(out=Gc[:], in0=Gc[:], in1=ff[:], op=ALU.add), ["Gc", ffk], ["Gc"])
        for d in range(2):
            init_state(d, (Hin[d], f"Hin{d}"))
            sweep(S0, TS // 128, d, True)
        P.barrier()
    if stop_after == "B2":
        return finish(nc, es, P, sems, None)

    P.mute = False
    relb_d = din("relb", [33, 16])
    onehot_d = din("onehot", [33, 3 * 384])
    kval_d = din("kval", [128, NKV])
    tvd = nc.dram_tensor("tvd", [16, 3 * 384], F32).ap()
    attnT = dscr("attnT", [D, NTOK], BF16)
    with ExitStack() as ph:
        def psb(name, shape, dt):
            return ph.enter_context(nc.sbuf_tensor("sb_" + name, list(shape), dt))
        anti_f = cst[:, C_ANTI:C_ANTI + 128]
        ones_f = cst[:, C_ONES:C_ONES + 128]
        relb = psb("relb", [33, 16], F32)
        oneh = psb("oneh", [33, 3 * 384], F32)
        kval = psb("kval", [128, NKV], F32)
        tvs = psb("tvs", [16, 3 * 384], F32)
        EB = psb("EB", [128, 3, 4, 2, 2, 2, 128], BF16)
        hks = Rot([(psb(f"hk{i}", [128, 2, 128], F32), f"hk{i}") for i in range(8)])
        P.dma(relb[:], relb_d, [], ["relb"], "ld_relb")
        P.dma(oneh[:], onehot_d, [], ["oneh"], "ld_oneh")
        P.dma(kval[:], kval_d, [], ["kval"], "ld_kval")
        for p in range(3):
            pb, pbk = bk(p)
            P.op("pe", lambda e, p=p, pb=pb: e.matmul(pb[0:16, 0:384], lhsT=relb[:], rhs=oneh[:, p * 384:(p + 1) * 384],
                                                     start=True, stop=True), ["relb", "oneh"], [pbk])
            P.op("dve", lambda e, p=p, pb=pb: e.tensor_copy(out=tvs[:, p * 384:(p + 1) * 384], in_=pb[0:16, 0:384]), [pbk], ["tvs"])
        P.dma(tvd, tvs[:], ["tvs"], ["tvd"], "st_tvs", q="pool")
        eb_banks = Rot([bk(4), bk(5), bk(6), bk(7)])
        for p in range(3):
            for h4 in range(4):
                hk_of = {}
                for hl in range(4):
                    h = h4 * 4 + hl
                    hk, hkk = hks.next()
                    src = bass.AP(tvd.tensor, h * 1152 + p * 384, [[1, 128], [128, 2], [1, 128]])
                    P.dma(hk[:], src, ["tvd"], [hkk], "ld_" + hkk)
                    hk_of[hl] = (hk, hkk)
                for kt in range(2):
                    pb, pbk = eb_banks.next()
                    for hl in range(4):
                        hk, hkk = hk_of[hl]
                        P.op("pe", lambda e, hk=hk, kt=kt, hl=hl, pb=pb: e.matmul(
                            pb[:, hl * 128:(hl + 1) * 128], lhsT=hk[:, kt, :], rhs=anti_f, start=True, stop=True),
                            [hkk, "cst"], [pbk])
                    P.op("act", lambda e, p=p, kt=kt, h4=h4, pb=pb: e.activation(
                        out=EB[:, p, h4, :, kt, :, :], in_=pb[:].rearrange("p (c e q) -> p e c q", c=2, e=2), func=AF.Copy, scale=8.0),
                        [pbk], [("EB", p, h4)])
        P.barrier()
        Vb = Rot([(psb(f"Vb{i}", [128, 17, 4, 65], BF16), f"Vb{i}") for i in range(4)])
        qTs = psb("qTs", [128, 2, TS], BF16)
        kTs = psb("kTs", [128, 2, TS + 2 * HALO], BF16)
        accT = psb("accT", [128, 4, TS], F32)
        pts = Rot([(psb(f"pt{i}", [128, 2, 2, 2, 128], BF16), f"pt{i}") for i in range(4)])
        obs = Rot([(psb(f"ob{i}", [64, 4, 512], BF16), f"ob{i}") for i in range(2)])
        sc_banks = Rot([(bk(0), bk(1)), (bk(2), bk(3)), (bk(4), bk(5))])
        ot_banks = Rot([bk(6), bk(7)])
        bc_banks = ot_banks
        LOOK = 2
        kvcol = 0
        norm_q = []

        def norm_prefetch():
            if norm_q and not norm_q[0]["done1"]:
                norm_q[0]["s1"]()
                norm_q[0]["done1"] = True

        def norm_pop():
            it = norm_q.pop(0)
            if not it["done1"]:
                it["s1"]()
            it["s2"]()
            norm_prefetch()
        for (seg, T, own_abs) in (("P", NPS, 0), ("S", TS, NPS + HALO)):
            kv0 = kvcol
            for hq in range(4):
                kvcol = kv0
                rows = lambda ap, hq=hq: ap[hq * 256:(hq + 1) * 256, :].rearrange("(c p) t -> p c t", p=128)
                P.dma(qTs[:, :, 0:T], rows(qT)[:, :, own_abs:own_abs + T], [], ["qTs"], "ld_qTs")
                P.dma(kTs[:, :, 0:T + 2048], rows(kT_full)[:, :, own_abs:own_abs + T + 2048], [], ["kTs"], "ld_kTs")
                blocks = []
                for p, dil in enumerate((1, 4, 16)):
                    nblk = T // (128 * dil)
                    for r in range(dil):
                        for ub in range(0, nblk, 16):
                            nb = min(16, nblk - ub)
                            unit = dict(ub=ub, nb=nb, buf=None)
                            for b in range(ub, ub + nb):
                                blocks.append(dict(p=p, dil=dil, r=r, b=b, nblk=nblk, unit=unit, first=(p == 0), kvc=kvcol))
                        kvcol += nblk + 1

                def stageA(B):
                    p, dil, r, b, nblk, unit = B["p"], B["dil"], B["r"], B["b"], B["nblk"], B["unit"]
                    if unit["buf"] is None:
                        vb, vbk = Vb.next()
                        unit["buf"] = (vb, vbk)
                        row0 = KPAD + own_abs + r - 64 * dil + 128 * dil * unit["ub"]
                        ntile = unit["nb"] + 1
                        for j0 in range(0, ntile, 9):
                            nj = min(9, ntile - j0)
                            src = bass.AP(vtok_full.tensor, (row0 + 128 * dil * j0) * VW + hq * 260,
                                          [[dil * VW, 128], [128 * dil * VW, nj], [1, 260]])
                            P.dma(vb[:, j0:j0 + nj, :, :].rearrange("p j h d -> p j (h d)"), src, [], [vbk], "ld_" + vbk)
                    (sx, sxk), (sy, syk) = sc_banks.next()
                    sbank = ((sx, sxk), (sy, syk))
                    q0 = r + dil * 128 * b
                    qsl = slice(q0, q0 + 127 * dil + 1, dil)
                    B["qsl"] = qsl
                    for e2 in range(2):
                        sb_, sbk_ = sbank[e2]
                        P.op("pe", lambda e, e2=e2, sb_=sb_, p=p, hq=hq: e.matmul(
                            sb_[:], lhsT=ident_b, rhs=EB[:, p, hq, e2, :, :, :].rearrange("p k c q -> p (k c q)"),
                            start=True, stop=False), ["cstb", ("EB", p, hq)], [sbk_])
                    for kt in range(2):
                        u0 = 1024 + r + dil * (128 * b - 64 + 128 * kt)
                        ksl = slice(u0, u0 + 127 * dil + 1, dil)
                        for c in range(2):
                            for e2 in range(2):
                                sb_, sbk_ = sbank[e2]
                                o0 = (kt * 2 + c) * 128
                                P.op("pe", lambda e, c=c, e2=e2, o0=o0, ksl=ksl, qsl=qsl, sb_=sb_, last=(kt == 1 and c == 1): e.matmul(
                                    sb_[:, o0:o0 + 128], lhsT=kTs[64 * e2:64 * e2 + 64, c, ksl],
                                    rhs=qTs[64 * e2:64 * e2 + 64, c, qsl], start=False, stop=last), ["qTs", "kTs"], [sbk_])
                    pt, ptk = pts.next()
                    B["pt"] = (pt, ptk)
                    interior = (b >= 1) and (b + 1 <= nblk - 1)
                    for e2 in range(2):
                        sb_, sbk_ = sbank[e2]
                        if interior:
                            P.op("act", lambda e, e2=e2, sb_=sb_, pt=pt: e.activation(
                                out=pt[:, e2, :, :, :].rearrange("p k c q -> p (k c q)"), in_=sb_[:], func=AF.Exp, scale=0.125),
                                [sbk_], [(ptk, e2)])
                        else:
                            for kt in range(2):
                                col = B["kvc"] + b + kt
                                P.op("act", lambda e, e2=e2, kt=kt, sb_=sb_, pt=pt, col=col: e.activation(
                                    out=pt[:, e2, kt, :, :].rearrange("p c q -> p (c q)"), in_=sb_[:, kt * 256:(kt + 1) * 256],
                                    func=AF.Exp, bias=kval[:, col:col + 1], scale=0.125), [sbk_, "kval"], [(ptk, e2)])

                def stageB(B):
                    b, unit = B["b"], B["unit"]
                    vb, vbk = unit["buf"]
                    pt, ptk = B["pt"]
                    qsl = B["qsl"]
                    ot, otk = ot_banks.next()
                    for c in range(2):
                        for e2 in range(2):
                            hl = 2 * c + e2
                            for kt in range(2):
                                jt = b - unit["ub"] + kt
                                P.op("pe", lambda e, hl=hl, c=c, e2=e2, kt=kt, jt=jt: e.matmul(
                                    ot[0:65, hl * 128:(hl + 1) * 128], lhsT=vb[:, jt, hl, :], rhs=pt[:, e2, kt, c, :],
                                    start=(kt == 0), stop=(kt == 1)), [vbk, (ptk, 0), (ptk, 1)], [otk])
                    acc_v = accT[0:65, :, qsl]
                    otv = ot[0:65, :].rearrange("p (h q) -> p h q", h=4)
                    akeys = [("accT", cbi) for cbi in range(qsl.start // 512, (qsl.stop - 1) // 512 + 1)]
                    if B["first"]:
                        P.op("dve", lambda e: e.tensor_copy(out=acc_v, in_=otv), [otk], akeys)
                    else:
                        P.op("dve", lambda e: e.tensor_tensor(out=acc_v, in0=acc_v, in1=otv, op=ALU.add), [otk] + akeys, akeys)

                nB = len(blocks)
                norm_prefetch()
                for i in range(nB + LOOK):
                    if i < nB:
                        stageA(blocks[i])
                    if i - LOOK >= 0:
                        Bk = blocks[i - LOOK]
                        if Bk["p"] == 0 and Bk["b"] % 4 == 0 and norm_q:
                            norm_pop()
                        stageB(Bk)
                while norm_q:
                    norm_pop()
                for cb in range(T // 512):
                    def norm_s1(cb=cb):
                        P.op("act", lambda e: e.activation(out=accT[64:65, :, cb * 512:(cb + 1) * 512],
                                                           in_=accT[64:65, :, cb * 512:(cb + 1) * 512], func=AF.Ln),
                             [("accT", cb)], [("accT", cb)])
                        P.op("act", lambda e: e.activation(out=accT[64:65, :, cb * 512:(cb + 1) * 512],
                                                           in_=accT[64:65, :, cb * 512:(cb + 1) * 512], func=AF.Exp, scale=-1.0),
                             [("accT", cb)], [("accT", cb)])

                    def norm_s2(cb=cb, hq=hq, own_abs=own_abs):
                        ob, obk = obs.next()
                        for hl in range(4):
                            pb, pbk = bc_banks.next()
                            P.op("pe", lambda e, hl=hl, pb=pb: e.matmul(
                                pb[0:64, :], lhsT=ones_f[64:65, 0:64], rhs=accT[64:65, hl, cb * 512:(cb + 1) * 512], start=True, stop=True),
                                [("accT", cb), "cst"], [pbk])
                            P.op("dve", lambda e, hl=hl, pb=pb, ob=ob: e.tensor_tensor(
                                out=ob[:, hl, :], in0=accT[0:64, hl, cb * 512:(cb + 1) * 512], in1=pb[0:64, :], op=ALU.mult),
                                [pbk, ("accT", cb)], [obk])
                        c0 = own_abs + cb * 512
                        P.dma(attnT[hq * 256:(hq + 1) * 256, c0:c0 + 512].rearrange("(hl p) t -> p hl t", p=64), ob[:], [obk],
                              [("attnT", hq, c0)], "st_" + obk, q="pool")
                    norm_q.append({"s1": norm_s1, "s2": norm_s2, "done1": False})
        while norm_q:
            norm_pop()
        assert kvcol == NKV, kvcol
        P.barrier()
    if stop_after == "B3":
        return finish(nc, es, P, sems, None)

    with ExitStack() as ph:
        C = ffn_ctx(ph, "c")
        def psb(name, shape, dt):
            return ph.enter_context(nc.sbuf_tensor("sb_c2" + name, list(shape), dt))
        ones_b = cstb[:, C_ONES:C_ONES + 128]
        atg = Rot([(psb(f"atg{i}", [128, 8, GT], BF16), f"atg{i}") for i in range(2)])
        smg = Rot([(psb(f"smg{i}", [128, 4, D], BF16), f"smg{i}") for i in range(2)])
        smT = psb("smT", [128, 8, GT], BF16)
        asq = psb("asq", [128, 8, GT], BF16)
        ars = psb("ars", [128, 8], F32)
        tr_banks = Rot([bk(0), bk(1)])
        oa_banks = Rot([bk(2), bk(3)])
        os_banks = Rot([bk(4), bk(5)])
        for (tok0, seg, gi, ng) in own_groups:
            xg, xkey = C.xgs.next()
            at, atk = atg.next()
            sm, smk = smg.next()
            P.dma(xg[:], x1s[tok0:tok0 + GT, :].rearrange("(t p) c -> p t c", p=128), [], [xkey], "ld_" + xkey)
            P.dma(at[:], attnT[:, tok0:tok0 + GT].rearrange("(c p) t -> p c t", p=128), [], [atk], "ld_" + atk)
            P.dma(sm[:], ssm_tok[tok0:tok0 + GT, :].rearrange("(t p) c -> p t c", p=128), [], [smk], "ld_" + smk)
            for t in range(4):
                bank, bkey = tr_banks.next()
                bv = bank[:].bitcast(BF16)
                for kc in range(8):
                    P.op("pe", lambda e, t=t, kc=kc, bv=bv, sm=sm: e.transpose(bv[:, kc * 128:(kc + 1) * 128],
                                                                               sm[:, t, kc * 128:(kc + 1) * 128], ident_b),
                         [smk, "cstb"], [bkey])
                P.op("act", lambda e, t=t, bv=bv: e.copy(out=smT[:, :, t * 128:(t + 1) * 128],
                                                         in_=bv.rearrange("p (k c) -> p k c", k=8)), [bkey], [("smT", t)])
            P.op("act", lambda e, at=at: e.activation(out=asq[:], in_=at[:], func=AF.Square), [atk], ["asq"])
            pss, pssk = bk(6)
            for t in range(4):
                for kc in range(8):
                    P.op("pe", lambda e, t=t, kc=kc: e.matmul(pss[:, t:t + 1], lhsT=asq[:, kc, t * 128:(t + 1) * 128],
                                                              rhs=ones_b[:, 0:1], start=(kc == 0), stop=(kc == 7)),
                         ["asq", "cstb"], [pssk])
            P.op("act", lambda e: e.activation(out=ars[:, 0:4], in_=pss[:, 0:4], func=AF.Ln, scale=1.0 / D, bias=EPS), [pssk], ["ars"])
            P.op("act", lambda e: e.activation(out=ars[:, 0:4], in_=ars[:, 0:4], func=AF.Exp, scale=-0.5), ["ars"], ["ars"])
            smT_keys = [("smT", t) for t in range(4)]
            for n in range(2):
                wa, wak = C.wload(wb["wout"][:, 0:8, n * 512:(n + 1) * 512], lambda b: b[:, :, :], ("wb", "wout"))
                ws, wsk = C.wload(wb["wout"][:, 8:16, n * 512:(n + 1) * 512], lambda b: b[:, :, :], ("wb", "wout"))
                for t in range(4):
                    pa, pak = oa_banks.next()
                    po, pok = os_banks.next()
                    for kc in range(8):
                        P.op("pe", lambda e, t=t, kc=kc, pa=pa, wa=wa, at=at: e.matmul(
                            pa[:], lhsT=at[:, kc, t * 128:(t + 1) * 128], rhs=wa[:, kc, :], start=(kc == 0), stop=(kc == 7)),
                            [atk, wak], [pak])
                    for kc in range(8):
                        P.op("pe", lambda e, t=t, kc=kc, po=po, ws=ws: e.matmul(
                            po[:], lhsT=smT[:, kc, t * 128:(t + 1) * 128], rhs=ws[:, kc, :], start=(kc == 0), stop=(kc == 7)),
                            smT_keys + [wsk], [pok])
                    xsl = xg[:, t, n * 512:(n + 1) * 512]
                    P.op("dve", lambda e, t=t, pa=pa, xsl=xsl: e.scalar_tensor_tensor(
                        out=xsl, in0=pa[:], scalar=ars[:, t:t + 1], in1=xsl, op0=ALU.mult, op1=ALU.add), [pak, "ars", xkey], [xkey])
                    P.op("dve", lambda e, po=po, xsl=xsl: e.tensor_tensor(out=xsl, in0=xsl, in1=po[:], op=ALU.add), [pok, xkey], [xkey])
            C.norm_transpose(xg, xkey, 0)
            C.ffn(xg, xkey, "w2g", "w2u", "w2d")
            orow = tok0 if seg == "P" else NPS + (tok0 - S0)
            P.dma(yout[orow:orow + GT, :].rearrange("(t p) c -> p t c", p=128), xg[:], [xkey], [("yout", orow)], "st_" + xkey, q="pool")
    return finish(nc, es, P, sems, None)


def finish(nc, es, P, sems, out_chans):
    chans = sorted({str(o.dma_chan) for o in P.ops if o.dma_chan is not None})
    assert len(chans) <= 97, len(chans)
    sems["dma"] = [nc.alloc_semaphore(name=f"d{i}") for i in range(len(chans))]
    P.build(sems)
    if out_chans is None:
        out_chans = list(P.chan_sem.keys())
    with nc.Block() as block:
        P.emit(block, out_chans=out_chans)
    if es is not None:
        es.close()
    return nc


def make_consts():
    c = np.zeros((128, NCONST), np.float32)
    i = np.arange(128)
    c[:, C_ID:C_ID + 128] = np.eye(128)
    c[:, C_TRIF:C_TRIF + 128] = (i[:, None] <= i[None, :])
    c[:, C_TRIB:C_TRIB + 128] = (i[:, None] >= i[None, :])
    c[:, C_BLK:C_BLK + 128] = ((i[:, None] // 64) == (i[None, :] // 64))
    c[:, C_ANTI:C_ANTI + 128] = ((i[:, None] + i[None, :]) == 127)
    c[:, C_ONES:C_ONES + 128] = 1.0
    return c


def make_vecs(inp):
    v = np.zeros((128, NV), np.float32)

    def colmajor(g):
        return np.asarray(g, np.float32).reshape(8, 128).T

    v[:, V_G1:V_G1 + 8] = colmajor(inp["ffn1_norm_g"][0])
    v[:, V_GMIX:V_GMIX + 8] = colmajor(inp["mix_norm_g"][0])
    v[:, V_G2:V_G2 + 8] = colmajor(inp["ffn2_norm_g"][0])
    v[:, V_GATT:V_GATT + 8] = colmajor(inp["attn_out_g"][0])
    v[:, V_GSSM:V_GSSM + 8] = colmajor(inp["ssm_out_g"][0])
    v[:, V_GQ] = np.tile(np.asarray(inp["q_norm_g"][0], np.float32), 2)
    v[:, V_GK] = np.tile(np.asarray(inp["k_norm_g"][0], np.float32), 2)
    cw = np.asarray(inp["conv_w"][0], np.float32)
    for tap in range(5):
        v[:, V_CW + tap * 16:V_CW + (tap + 1) * 16] = cw[tap].reshape(16, 128).T
    v[:, V_CB:V_CB + 16] = np.asarray(inp["conv_b"][0], np.float32).reshape(16, 128).T
    v[:, V_DTB:V_DTB + 32] = np.asarray(inp["dt_bias"][0], np.float32).reshape(1, 32)
    v[:, V_ALOG:V_ALOG + 32] = np.asarray(inp["a_log"][0], np.float32).reshape(1, 32)
    v[:, V_DSK:V_DSK + 16] = np.asarray(inp["d_skip"][0], np.float32).reshape(1, 16)
    return v


def _t5_bucket(rel):
    nb = 16
    max_exact = 8
    n = np.abs(rel)
    large = max_exact + (np.log(np.maximum(n, 1) / max_exact) / np.log(1024 / max_exact) * (nb - max_exact)).astype(np.int32)
    large = np.minimum(large, nb - 1)
    return (np.where(rel > 0, nb, 0) + np.where(n < max_exact, n, large)).astype(np.int32)


def make_onehot():
    oh = np.zeros((33, 3 * 384), np.float32)
    for p, dil in enumerate((1, 4, 16)):
        j = np.arange(384)
        delta = j - 191
        bkt = _t5_bucket(delta * dil)
        inw = np.abs(delta) <= 64
        for jj in range(384):
            if inw[jj]:
                oh[bkt[jj], p * 384 + jj] = 1.0
            else:
                oh[32, p * 384 + jj] = NEG
    return oh


def make_kval(core):
    kv = np.zeros((128, NKV), np.float32)
    k = np.arange(128)
    for ci, (seg, dil, r, jt) in enumerate(KV_COLS):
        if seg == "P":
            start, slen = 0, NPS
        else:
            start, slen = (core % 4) * TS, 4 * TS
        g = start + r + dil * (128 * jt - 64 + k)
        kv[:, ci] = np.where((g >= 0) & (g < slen), 0.0, NEG)
    return kv


def make_in_maps(inp):
    xp = np.asarray(inp["x_prompt"], np.float32)
    xs = np.asarray(inp["x_sample"], np.float32)
    consts = make_consts()
    vecs = make_vecs(inp)
    shared = {
        "w1g": np.ascontiguousarray(inp["ffn1_w_gate"][0], dtype=np.float32),
        "w1u": np.ascontiguousarray(inp["ffn1_w_up"][0], dtype=np.float32),
        "w1d": np.ascontiguousarray(inp["ffn1_w_down"][0], dtype=np.float32),
        "win": np.ascontiguousarray(inp["w_in"][0], dtype=np.float32),
        "wout": np.ascontiguousarray(inp["w_out"][0], dtype=np.float32),
        "w2g": np.ascontiguousarray(inp["ffn2_w_gate"][0], dtype=np.float32),
        "w2u": np.ascontiguousarray(inp["ffn2_w_up"][0], dtype=np.float32),
        "w2d": np.ascontiguousarray(inp["ffn2_w_down"][0], dtype=np.float32),
        "vecs": vecs, "consts": consts,
        "relb": np.concatenate([np.asarray(inp["rel_bias"], np.float32), np.ones((1, 16), np.float32)], 0),
        "onehot": make_onehot(),
    }
    maps = []
    for c in range(8):
        s, j = c // 4, c % 4
        x = np.zeros((NTOK, D), np.float32)
        x[0:NPS] = xp[c]
        lo = j * TS - HALO
        hi = (j + 1) * TS + HALO
        slo, shi = max(lo, 0), min(hi, xs.shape[1])
        x[NPS + (slo - lo):NPS + (shi - lo)] = xs[s, slo:shi]
        m = dict(shared)
        m["xin"] = x
        sel = np.zeros((128, 4), np.float32)
        sel[:, j] = 1.0
        m["sel"] = sel
        m["kval"] = make_kval(c)
        maps.append(m)
    return maps


_CACHE = {}


def kernel(**inputs):
    if "nc" not in _CACHE:
        _CACHE["nc"] = build_program()
    nc = _CACHE["nc"]
    maps = make_in_maps(inputs)
    res = run_bass_kernel_spmd(nc, maps, core_ids=list(range(8)))
    yp = np.zeros((8, NPS, D), np.float32)
    ys = np.zeros((2, 4 * TS, D), np.float32)
    for c in range(8):
        y = res.results[c]["yout"]
        yp[c] = y[0:NPS]
        ys[c // 4, (c % 4) * TS:(c % 4 + 1) * TS] = y[NPS:NPS + TS]
    return (yp, ys)
```

```python
import numpy as np
from contextlib import ExitStack
import ml_dtypes
import concourse.bass as bass
import concourse.mybir as mybir
from concourse.bass_utils import run_bass_kernel_spmd

F32 = mybir.dt.float32
BF16 = mybir.dt.bfloat16
AF = mybir.ActivationFunctionType
ALU = mybir.AluOpType
AX = mybir.AxisListType

COMPUTE = ("pe", "act", "dve", "pool")

D = 1024
FF = 2816
MC = FF // 128
IN_DIM = 6176
NPS = 2048
HALO = 1024
TS = 4096
NSS = TS + 2 * HALO
NTOK = NPS + NSS
NOWN = NPS + TS
EPS = 1e-6
NEG = -30000.0
GT = 512
KPAD = 1024
VW = 16 * 65

V_G1, V_GMIX, V_G2, V_GATT, V_GSSM = 0, 8, 16, 24, 32
V_GQ, V_GK = 40, 41
V_CW = 42
V_CB = 122
V_DTB = 138
V_ALOG = 170
V_DSK = 202
NV = 218
C_ID, C_TRIF, C_TRIB, C_BLK, C_ANTI, C_ONES = 0, 128, 256, 384, 512, 640
NCONST = 768
def _kv_layout():
    cols = []
    for seg, T in (("P", NPS), ("S", TS)):
        for p, dil in enumerate((1, 4, 16)):
            nblk = T // (128 * dil)
            for r in range(dil):
                for jt in range(nblk + 1):
                    cols.append((seg, dil, r, jt))
    return cols
KV_COLS = _kv_layout()
NKV = len(KV_COLS)


class Op:
    __slots__ = ("eng", "fn", "deps", "idx", "signal", "dma_chan", "cnt", "inc")

    def __init__(self, eng, fn, idx, dma_chan=None, inc=16):
        self.eng = eng
        self.inc = inc
        self.fn = fn
        self.idx = idx
        self.deps = set()
        self.signal = False
        self.dma_chan = dma_chan
        self.cnt = 0


class Prog:
    def __init__(self, nc):
        self.nc = nc
        self.ops = []
        self.last_w = {}
        self.readers = {}
        self.barrier_deps = set()
        self.last_eng = {}
        self.last_chan = {}

    mute = False

    def op(self, eng, fn, reads=(), writes=(), chan=None, inc=16):
        if self.mute:
            return None
        o = Op(eng, fn, len(self.ops), dma_chan=chan, inc=inc)
        self.ops.append(o)
        o.deps |= self.barrier_deps
        for r in reads:
            w = self.last_w.get(r)
            if w is not None:
                o.deps.add(w)
        for r in writes:
            w = self.last_w.get(r)
            if w is not None:
                o.deps.add(w)
            for rd in self.readers.get(r, ()):
                o.deps.add(rd)
        for r in reads:
            self.readers.setdefault(r, []).append(o.idx)
        for r in writes:
            self.last_w[r] = o.idx
            self.readers[r] = []
        o.deps.discard(o.idx)
        if chan is not None:
            self.last_chan[chan] = o.idx
        else:
            self.last_eng[eng] = o.idx
        return o

    def barrier(self):
        self.barrier_deps = set(self.last_eng.values()) | set(self.last_chan.values())

    def dma(self, out, in_, reads, writes, chan, q="sp"):
        return self.op(q, lambda e: e.dma_start(out=out, in_=in_), reads, writes, chan=chan)

    def build(self, sems):
        ops = self.ops
        eng_sem = {e: sems[e] for e in COMPUTE}
        chan_sem = {}
        free = list(sems["dma"])
        chan_cnt = {}
        for o in ops:
            if o.dma_chan is not None:
                if o.dma_chan not in chan_sem:
                    chan_sem[o.dma_chan] = free.pop()
                chan_cnt[o.dma_chan] = chan_cnt.get(o.dma_chan, 0) + o.inc
                o.cnt = chan_cnt[o.dma_chan]
        waited = {}
        need = []
        for o in ops:
            keep = []
            for d in sorted(o.deps):
                od = ops[d]
                if od.dma_chan is not None:
                    key = (o.eng, "c", od.dma_chan)
                    if waited.get(key, -1) >= od.cnt:
                        continue
                    waited[key] = od.cnt
                    keep.append(d)
                else:
                    if od.eng == o.eng and o.eng == "pe" and o.dma_chan is None:
                        continue
                    key = (o.eng, "e", od.eng)
                    if waited.get(key, -1) >= d:
                        continue
                    waited[key] = d
                    keep.append(d)
                    od.signal = True
            need.append(keep)
        cnt = {e: 0 for e in COMPUTE}
        for o in ops:
            if o.dma_chan is None and o.signal:
                cnt[o.eng] += 1
                o.cnt = cnt[o.eng]
        streams = {}
        for o, keep in zip(ops, need):
            waits = {}
            for d in keep:
                od = ops[d]
                s = chan_sem[od.dma_chan] if od.dma_chan is not None else eng_sem[od.eng]
                k = id(s)
                if k not in waits or waits[k][1] < od.cnt:
                    waits[k] = (s, od.cnt)
            streams.setdefault(o.eng, []).append((o, list(waits.values())))
        self.streams = streams
        self.chan_sem = chan_sem
        self.chan_cnt = chan_cnt
        self.eng_sem = eng_sem

    def emit(self, block, out_chans=()):
        engmap = {"pe": block.tensor, "act": block.scalar, "dve": block.vector,
                  "pool": block.gpsimd, "sp": block.sync}
        streams = self.streams
        chan_sem, chan_cnt, eng_sem = self.chan_sem, self.chan_cnt, self.eng_sem

        def make(ename):
            lst = streams.get(ename, [])

            def body(e):
                for o, waits in lst:
                    for s, c in waits:
                        e.wait_ge(s, c)
                    ins = o.fn(e)
                    if o.dma_chan is not None:
                        ins.then_inc(chan_sem[o.dma_chan], o.inc)
                    elif o.signal:
                        ins.then_inc(eng_sem[o.eng], 1)
                if ename == "sp":
                    for ch in out_chans:
                        if ch in chan_sem:
                            e.wait_ge(chan_sem[ch], chan_cnt[ch])
            return body

        for ename in ("sp", "pe", "act", "dve", "pool"):
            if ename == "sp" or streams.get(ename):
                engmap[ename](make(ename))


class Rot:
    def __init__(self, items):
        self.items = items
        self.i = 0

    def next(self):
        it = self.items[self.i % len(self.items)]
        self.i += 1
        return it


def build_program(debug=(), stop_after="C"):
    nc = bass.Bass("TRN2", target_bir_lowering=False)
    es = ExitStack()

    def din(name, shape, dt=F32):
        return nc.dram_tensor(name, list(shape), dt, kind="ExternalInput").ap()

    def dscr(name, shape, dt):
        if name in debug:
            return nc.dram_tensor(name, list(shape), dt, kind="ExternalOutput").ap()
        return nc.dram_tensor(name, list(shape), dt).ap()

    xin = din("xin", [NTOK, D])
    w_src = {
        "w1g": din("w1g", [D, FF]), "w1u": din("w1u", [D, FF]), "w1d": din("w1d", [FF, D]),
        "win": din("win", [D, IN_DIM]), "wout": din("wout", [2 * D, D]),
        "w2g": din("w2g", [D, FF]), "w2u": din("w2u", [D, FF]), "w2d": din("w2d", [FF, D]),
    }
    vecs_d = din("vecs", [128, NV])
    consts_d = din("consts", [128, NCONST])
    yout = nc.dram_tensor("yout", [NOWN, D], F32, kind="ExternalOutput").ap()

    wb = {
        "w1g": dscr("wb1g", [128, 8, FF], BF16), "w1u": dscr("wb1u", [128, 8, FF], BF16),
        "w1d": dscr("wb1d", [128, MC, D], BF16), "win": dscr("wbin", [128, 8, IN_DIM], BF16),
        "wout": dscr("wbout", [128, 16, D], BF16),
        "w2g": dscr("wb2g", [128, 8, FF], BF16), "w2u": dscr("wb2u", [128, 8, FF], BF16),
        "w2d": dscr("wb2d", [128, MC, D], BF16),
    }
    x1s = dscr("x1s", [NTOK, D], F32)
    qT = dscr("qT", [D, NTOK], BF16)
    kT_full = dscr("kT", [D, KPAD + NTOK], BF16)
    kT = kT_full[:, KPAD:]
    vtok_full = dscr("vtok", [KPAD + NTOK, VW], BF16)
    vtok = vtok_full[KPAD:, :]
    ztok = dscr("ztok", [NTOK, D], BF16)
    dtr = dscr("dtr", [NTOK, 32], F32)
    xbcT = dscr("xbcT", [2 * D, NTOK], BF16)

    def sb(name, shape, dt):
        return es.enter_context(nc.sbuf_tensor("sb_" + name, list(shape), dt))

    sems = {e: es.enter_context(nc.semaphore("s_" + e)) for e in COMPUTE}
    P = Prog(nc)

    banks = [es.enter_context(nc.psum_tensor(f"bank{i}", [128, 512], F32)) for i in range(8)]

    def bk(i):
        return banks[i], ("ps", i)

    vecs = sb("vecs", [128, NV], F32)
    cst = sb("cst", [128, NCONST], F32)
    cstb = sb("cstb", [128, NCONST], BF16)
    P.dma(vecs[:], vecs_d, [], ["vecs"], "ld_vecs")
    P.dma(cst[:], consts_d, [], ["cst"], "ld_cst")
    P.op("dve", lambda e: e.tensor_copy(out=cstb[:], in_=cst[:]), ["cst"], ["cstb"])
    ident_b = cstb[:, C_ID:C_ID + 128]
    blk_b = cstb[:, C_BLK:C_BLK + 128]

    import os as _os
    if _os.environ.get("SKIP_PRE"):
        P.mute = True
    with ExitStack() as ph:
        def psb(name, shape, dt):
            return ph.enter_context(nc.sbuf_tensor("sb_" + name, list(shape), dt))
        CW = 2048
        st32 = [psb(f"st32_{i}", [128, CW], F32) for i in range(4)]
        st16 = [psb(f"st16_{i}", [128, CW], BF16) for i in range(4)]
        zt = psb("zt", [128, 8, VW], BF16)
        P.op("pool", lambda e: e.memset(zt[:], 0.0), [], ["zt"])
        P.dma(kT_full[:, 0:KPAD].rearrange("(c p) t -> p c t", p=128), zt[:, :, 0:1024], ["zt"], ["kpad"], "st_zt", q="pool")
        P.dma(vtok_full[0:KPAD, :].rearrange("(t p) c -> p t c", p=128), zt[:], ["zt"], ["vpad"], "st_zt", q="pool")
        folds = {"w1g": V_G1, "w1u": V_G1, "win": V_GMIX, "w2g": V_G2, "w2u": V_G2, "wout": V_GATT}
        slot = 0
        for wname, src in w_src.items():
            if wname not in ("w1g", "w1u"):
                continue
            K, N = src.shape
            for kc in range(K // 128):
                for c0 in range(0, N, CW):
                    c1 = min(N, c0 + CW)
                    s = slot % 4
                    slot += 1
                    a32, a16 = st32[s], st16[s]
                    P.dma(a32[:, 0:c1 - c0], src[kc * 128:(kc + 1) * 128, c0:c1], [], [f"st32_{s}"], f"ldw{s}")
                    fold = folds.get(wname)
                    eng = ("dve", "act")[s % 2]
                    if fold is None:
                        if eng == "act":
                            P.op(eng, lambda e, a16=a16, a32=a32, n=c1 - c0: e.copy(out=a16[:, 0:n], in_=a32[:, 0:n]),
                                 [f"st32_{s}"], [f"st16_{s}"])
                        else:
                            P.op(eng, lambda e, a16=a16, a32=a32, n=c1 - c0: e.tensor_copy(out=a16[:, 0:n], in_=a32[:, 0:n]),
                                 [f"st32_{s}"], [f"st16_{s}"])
                    else:
                        col = vecs[:, fold + kc:fold + kc + 1]
                        if eng == "act":
                            P.op(eng, lambda e, a16=a16, a32=a32, n=c1 - c0, col=col: e.activation(
                                out=a16[:, 0:n], in_=a32[:, 0:n], func=AF.Copy, scale=col),
                                [f"st32_{s}", "vecs"], [f"st16_{s}"])
                        else:
                            P.op(eng, lambda e, a16=a16, a32=a32, n=c1 - c0, col=col: e.tensor_scalar(
                                out=a16[:, 0:n], in0=a32[:, 0:n], scalar1=col, scalar2=None, op0=ALU.mult),
                                [f"st32_{s}", "vecs"], [f"st16_{s}"])
                    P.dma(wb[wname][:, kc, c0:c1], a16[:, 0:c1 - c0], [f"st16_{s}"], [("wb", wname)], f"stw{s}")
        P.barrier()
    if stop_after == "W":
        return finish(nc, es, P, sems, ["stw0", "stw1", "stw2"])

    def ffn_phase(ph, groups, wg, wu, wd, gcol_unused, x_src_fn, after_ffn_fn, tagp):
        pass

    from types import SimpleNamespace

    def ffn_ctx(ph, tag):
        def psb(name, shape, dt):
            return ph.enter_context(nc.sbuf_tensor("sb_" + tag + name, list(shape), dt))
        NWB = 6
        wbufs = Rot([(psb(f"wbuf{i}", [128, 8, 512], BF16), f"wbuf{i}") for i in range(NWB)])
        xgs = Rot([(psb(f"xg{i}", [128, 4, D], F32), f"xg{i}") for i in range(2)])
        xn = psb("xn", [128, 4, D], BF16)
        xnT = psb("xnT", [128, 8, GT], BF16)
        xnT2 = psb("xnT2", [128, 8, GT], BF16) if tag == "a" else None
        hT = psb("hT", [128, MC, GT], BF16)
        sils = Rot([(psb(f"sil{i}", [128, GT], F32), f"sil{i}") for i in range(2)])
        ssq = psb("ssq", [128, 8], F32)
        rstd = psb("rstd", [128, 8], F32)
        junk = psb("junk", [128, D], BF16)
        sqs = Rot([(psb(f"sq{i}", [128, GT], BF16), f"sq{i}") for i in range(2)])
        lns = Rot([(psb(f"ln{i}", [128, GT], F32), f"ln{i}") for i in range(2)])
        qns = Rot([(psb(f"qn{i}", [128, GT], BF16), f"qn{i}") for i in range(3)])
        vst = psb("vst", [128, 4, 16, 65], BF16)
        if tag == "a":
            P.op("pool", lambda e: e.memset(vst[:, :, :, 64:65], 1.0), [], ["vst"])
        zst = psb("zst", [128, 4, D], BF16)
        dst = psb("dst", [128, 4, 32], F32)
        tr_banks = Rot([bk(0), bk(1)])
        g_banks = Rot([bk(2), bk(3)])
        u_banks = Rot([bk(4), bk(5)])
        d_banks = Rot([bk(6), bk(7)])
        fm_banks = Rot([bk(2), bk(3), bk(4), bk(5)])
        ev = Rot(["dve", "act"])

        def wload(src_ap, shape_view, rkey=None):
            buf, key = wbufs.next()
            view = shape_view(buf)
            P.dma(view, src_ap, [rkey] if rkey is not None else [], [key], "ld_" + key)
            return buf, key

        def norm_transpose(xg, xkey, col0, xT=None, xname="xnT"):
            xT = xnT if xT is None else xT
            for t in range(4):
                P.op("act", lambda e, t=t: e.activation(out=junk[:], in_=xg[:, t, :], func=AF.Square,
                                                        accum_out=ssq[:, col0 + t:col0 + t + 1]),
                     [xkey], [("ssq", col0 + t)])
            P.op("act", lambda e: e.activation(out=rstd[:, col0:col0 + 4], in_=ssq[:, col0:col0 + 4], func=AF.Ln,
                                               scale=1.0 / D, bias=EPS),
                 [("ssq", col0 + t) for t in range(4)], [("rstd", col0)])
            P.op("act", lambda e: e.activation(out=rstd[:, col0:col0 + 4], in_=rstd[:, col0:col0 + 4], func=AF.Exp,
                                               scale=-0.5),
                 [("rstd", col0)], [("rstd", col0)])
            for t in range(4):
                P.op("dve", lambda e, t=t: e.tensor_scalar(out=xn[:, t, :], in0=xg[:, t, :],
                                                           scalar1=rstd[:, col0 + t:col0 + t + 1], scalar2=None,
                                                           op0=ALU.mult),
                     [xkey, ("rstd", col0)], [("xn", t)])
                bank, bkey = tr_banks.next()
                bv = bank[:].bitcast(BF16)
                for kc in range(8):
                    P.op("pe", lambda e, t=t, kc=kc, bv=bv: e.transpose(bv[:, kc * 128:(kc + 1) * 128],
                                                                        xn[:, t, kc * 128:(kc + 1) * 128], ident_b),
                         [("xn", t), "cstb"], [bkey])
                eng = ev.next()
                src = bv.rearrange("p (k c) -> p k c", k=8)
                dstv = xT[:, :, t * 128:(t + 1) * 128]
                if eng == "act":
                    P.op("act", lambda e, src=src, dstv=dstv: e.copy(out=dstv, in_=src), [bkey], [(xname, t)])
                else:
                    P.op("dve", lambda e, src=src, dstv=dstv: e.tensor_copy(out=dstv, in_=src), [bkey], [(xname, t)])

        xnT_keys = [("xnT", t) for t in range(4)]
        xnT2_keys = [("xnT2", t) for t in range(4)]

        def ffn(xg, xkey, wgn, wun, wdn):
            ffn_gu(wgn, wun)
            ffn_down(xg, xkey, wdn)

        def ffn_gu(wgn, wun):
            for mb in range(0, MC, 4):
                nm = min(4, MC - mb)
                gb, gkey = wload(wb[wgn][:, :, mb * 128:(mb + nm) * 128], lambda b: b[:, :, 0:nm * 128], ("wb", wgn))
                ub, ukey = wload(wb[wun][:, :, mb * 128:(mb + nm) * 128], lambda b: b[:, :, 0:nm * 128], ("wb", wun))
                for j in range(nm):
                    m = mb + j
                    pg, pgk = g_banks.next()
                    pu, puk = u_banks.next()
                    for kc in range(8):
                        P.op("pe", lambda e, kc=kc, j=j, pg=pg, gb=gb: e.matmul(
                            pg[:], lhsT=gb[:, kc, j * 128:(j + 1) * 128], rhs=xnT[:, kc, :],
                            start=(kc == 0), stop=(kc == 7)), [gkey] + xnT_keys, [pgk])
                    for kc in range(8):
                        P.op("pe", lambda e, kc=kc, j=j, pu=pu, ub=ub: e.matmul(
                            pu[:], lhsT=ub[:, kc, j * 128:(j + 1) * 128], rhs=xnT[:, kc, :],
                            start=(kc == 0), stop=(kc == 7)), [ukey] + xnT_keys, [puk])
                    sl, slk = sils.next()
                    P.op("act", lambda e, sl=sl, pg=pg: e.activation(out=sl[:], in_=pg[:], func=AF.Silu), [pgk], [slk])
                    P.op("dve", lambda e, sl=sl, pu=pu, m=m: e.tensor_tensor(out=hT[:, m, :], in0=sl[:], in1=pu[:],
                                                                             op=ALU.mult), [slk, puk], [("hT", m)])

        def ffn_down(xg, xkey, wdn):
            hkeys = [("hT", m) for m in range(MC)]
            for n in range(2):
                blocks = []
                for mb in range(0, MC, 8):
                    nm = min(8, MC - mb)
                    blocks.append((mb, nm) + wload(wb[wdn][:, mb:mb + nm, n * 512:(n + 1) * 512],
                                                  lambda b, nm=nm: b[:, 0:nm, :], ("wb", wdn)))
                for t in range(4):
                    pd, pdk = d_banks.next()
                    for (mb, nm, dbuf, dkey) in blocks:
                        for j in range(nm):
                            m = mb + j
                            P.op("pe", lambda e, m=m, j=j, t=t, pd=pd, dbuf=dbuf: e.matmul(
                                pd[:], lhsT=hT[:, m, t * 128:(t + 1) * 128], rhs=dbuf[:, j, :],
                                start=(m == 0), stop=(m == MC - 1)), [dkey] + hkeys, [pdk])
                    P.op("dve", lambda e, t=t, n=n, pd=pd: e.scalar_tensor_tensor(
                        out=xg[:, t, n * 512:(n + 1) * 512], in0=pd[:], scalar=0.5,
                        in1=xg[:, t, n * 512:(n + 1) * 512], op0=ALU.mult, op1=ALU.add), [pdk, xkey], [xkey])


        return SimpleNamespace(**{k: v for k, v in locals().items()})

    with ExitStack() as ph:
        C = ffn_ctx(ph, "a")
        xgs, norm_transpose, ffn, wload, fm_banks, d_banks = C.xgs, C.norm_transpose, C.ffn, C.wload, C.fm_banks, C.d_banks
        sqs, lns, qns, vst, zst, dst, ev, xnT, xnT_keys = C.sqs, C.lns, C.qns, C.vst, C.zst, C.dst, C.ev, C.xnT, C.xnT_keys
        ngroups = NTOK // GT
        xnT2a, xnT2a_keys = C.xnT2, C.xnT2_keys
        ffn_gu_a, ffn_down_a = C.ffn_gu, C.ffn_down
        pending = None
        lw32 = [ph.enter_context(nc.sbuf_tensor(f"sb_lw32_{i}", [128, 1024], F32)) for i in range(3)]
        lw16 = [ph.enter_context(nc.sbuf_tensor(f"sb_lw16_{i}", [128, 1024], BF16)) for i in range(3)]
        late = []
        for wname in ("wout", "w2g", "w2u", "w2d"):
            K, N = w_src[wname].shape
            fold = {"wout": V_GATT, "w2g": V_G2, "w2u": V_G2}.get(wname)
            for kc in range(K // 128):
                for c0 in range(0, N, 1024):
                    late.append((wname, kc, c0, min(N, c0 + 1024), fold))
        late_state = {"i": 0, "loaded": 0}

        def late_load(k):
            wname, kc, c0, c1, fold = late[k]
            sl = k % 3
            P.dma(lw32[sl][:, 0:c1 - c0], w_src[wname][kc * 128:(kc + 1) * 128, c0:c1], [], [f"lw32_{sl}"], f"ld_lw{sl}", q="act")

        def late_step(n):
            for _ in range(n):
                k = late_state["i"]
                if k >= len(late):
                    return
                while late_state["loaded"] < min(len(late), k + 2):
                    late_load(late_state["loaded"])
                    late_state["loaded"] += 1
                wname, kc, c0, c1, fold = late[k]
                sl = k % 3
                a32, a16, nn = lw32[sl], lw16[sl], c1 - c0
                if fold is None:
                    P.op("act", lambda e, a32=a32, a16=a16, nn=nn: e.copy(out=a16[:, 0:nn], in_=a32[:, 0:nn]),
                         [f"lw32_{sl}"], [f"lw16_{sl}"])
                else:
                    col = vecs[:, fold + kc:fold + kc + 1]
                    P.op("act", lambda e, a32=a32, a16=a16, nn=nn, col=col: e.activation(
                        out=a16[:, 0:nn], in_=a32[:, 0:nn], func=AF.Copy, scale=col),
                        [f"lw32_{sl}", "vecs"], [f"lw16_{sl}"])
                P.dma(wb[wname][:, kc, c0:c1], a16[:, 0:nn], [f"lw16_{sl}"], [("wb", wname)], f"st_lw{sl}", q="act")
                late_state["i"] += 1

        GX = {}

        def s_load(g):
            xg, xkey = xgs.next()
            GX[g] = (xg, xkey)
            tok0 = g * GT
            P.dma(xg[:], xin[tok0:tok0 + GT, :].rearrange("(t p) c -> p t c", p=128), [], [xkey], "ld_" + xkey)
            late_step(6)

        def s_rest(g):
            pending = None
            xg, xkey = GX[g]
            tok0 = g * GT
            P.dma(x1s[tok0:tok0 + GT, :].rearrange("(t p) c -> p t c", p=128), xg[:], [xkey], [("x1s", g)],
                  "st_" + xkey, q="pool")
            norm_transpose(xg, xkey, 4, xnT2a, "xnT2")
            sg = g - NPS // GT
            halo = (0 <= sg < HALO // GT) or (sg >= (HALO + TS) // GT)
            csl = slice(0, GT)
            if halo:
                if sg == HALO // GT - 1:
                    csl = slice(GT - 2, GT)
                elif sg == (HALO + TS) // GT:
                    csl = slice(0, 2)
                else:
                    csl = None
            for cb in range(8):
                if halo and (cb < 2 or (cb >= 4 and csl is None)):
                    continue
                col0 = cb * 512 if cb < 4 else 4096 + (cb - 4) * 512
                wbf, wkey = wload(wb["win"][:, :, col0:col0 + 512], lambda b: b[:, :, :], ("wb", "win"))
                for j in range(4):
                    ch = cb * 4 + j
                    pb, pbk = fm_banks.next()
                    msl = csl if ch >= 16 else slice(0, GT)
                    for kc in range(8):
                        P.op("pe", lambda e, kc=kc, j=j, pb=pb, wbf=wbf, msl=msl: e.matmul(
                            pb[:, msl], lhsT=wbf[:, kc, j * 128:(j + 1) * 128], rhs=xnT2a[:, kc, msl],
                            start=(kc == 0), stop=(kc == 7)), [wkey] + xnT2a_keys, [pbk])
                    def post(ch=ch, pb=pb, pbk=pbk, msl=msl):
                        if ch < 16:
                            sq, sqk = sqs.next()
                            ln, lnk = lns.next()
                            qn, qnk = qns.next()
                            ps, psk = d_banks.next()
                            gcol = vecs[:, V_GQ:V_GQ + 1] if ch < 8 else vecs[:, V_GK:V_GK + 1]
                            P.op("act", lambda e, sq=sq, pb=pb: e.activation(out=sq[:], in_=pb[:], func=AF.Square), [pbk], [sqk])
                            P.op("pe", lambda e, sq=sq, ps=ps: e.matmul(ps[:], lhsT=blk_b, rhs=sq[:], start=True, stop=True),
                                 [sqk, "cstb"], [psk])
                            P.op("act", lambda e, ln=ln, ps=ps: e.activation(out=ln[:], in_=ps[:], func=AF.Ln,
                                                                             scale=1.0 / 64, bias=EPS), [psk], [lnk])
                            P.op("act", lambda e, ln=ln: e.activation(out=ln[:], in_=ln[:], func=AF.Exp, scale=-0.5),
                                 [lnk], [lnk])
                            P.op("dve", lambda e, qn=qn, pb=pb, ln=ln, gcol=gcol: e.scalar_tensor_tensor(
                                out=qn[:], in0=pb[:], scalar=gcol, in1=ln[:], op0=ALU.mult, op1=ALU.mult),
                                [pbk, lnk, "vecs"], [qnk])
                            dstT = qT if ch < 8 else kT
                            r0 = (ch % 8) * 128
                            P.dma(dstT[r0:r0 + 128, tok0:tok0 + GT], qn[:], [qnk], [("qk", ch, g)], "st_" + qnk, q="pool")
                        else:
                            qn, qnk = qns.next()
                            eng = ev.next()
                            if eng == "act":
                                P.op("act", lambda e, qn=qn, pb=pb: e.copy(out=qn[:, msl], in_=pb[:, msl]), [pbk], [qnk])
                            else:
                                P.op("dve", lambda e, qn=qn, pb=pb: e.tensor_copy(out=qn[:, msl], in_=pb[:, msl]), [pbk], [qnk])
                            r0 = (ch - 16) * 128
                            P.dma(xbcT[r0:r0 + 128, tok0 + msl.start:tok0 + msl.stop], qn[:, msl], [qnk], [("xbc", ch, g)],
                                  "st_" + qnk, q="pool")

                    if pending is not None:
                        pending()
                    pending = post
            if pending is not None:
                pending()
                pending = None
            for blk in range(5):
                if halo and blk >= 2:
                    continue
                if blk < 4:
                    col0 = 2048 + blk * 512
                    ncol = 512
                else:
                    col0 = 6144
                    ncol = 32
                wbf, wkey = wload(wb["win"][:, :, col0:col0 + ncol], lambda b, ncol=ncol: b[:, :, 0:ncol], ("wb", "win"))
                for t in range(4):
                    pb, pbk = fm_banks.next()
                    for kc in range(8):
                        P.op("pe", lambda e, kc=kc, t=t, pb=pb, wbf=wbf, ncol=ncol: e.matmul(
                            pb[:, 0:ncol], lhsT=xnT2a[:, kc, t * 128:(t + 1) * 128], rhs=wbf[:, kc, 0:ncol],
                            start=(kc == 0), stop=(kc == 7)), [wkey] + xnT2a_keys, [pbk])
                    if blk < 2:
                        dsti, dkey = vst[:, t, blk * 8:(blk + 1) * 8, 0:64], "vst"
                    elif blk < 4:
                        dsti, dkey = zst[:, t, (blk - 2) * 512:(blk - 1) * 512], "zst"
                    else:
                        dsti, dkey = dst[:, t, :], "dst"
                    eng = ev.next()
                    srcv = pb[:, 0:ncol].rearrange("p (h d) -> p h d", d=64) if blk < 2 else pb[:, 0:ncol]
                    if eng == "act":
                        P.op("act", lambda e, dsti=dsti, srcv=srcv: e.copy(out=dsti, in_=srcv), [pbk], [dkey])
                    else:
                        P.op("dve", lambda e, dsti=dsti, srcv=srcv: e.tensor_copy(out=dsti, in_=srcv), [pbk], [dkey])
            P.dma(vtok[tok0:tok0 + GT, :].rearrange("(t p) c -> p t c", p=128), vst[:].rearrange("p t h d -> p t (h d)"), ["vst"], [("vtok", g)], "st_vst", q="pool")
            if not halo:
                P.dma(ztok[tok0:tok0 + GT, :].rearrange("(t p) c -> p t c", p=128), zst[:], ["zst"], [("ztok", g)], "st_zst", q="pool")
                P.dma(dtr[tok0:tok0 + GT, :].rearrange("(t p) c -> p t c", p=128), dst[:], ["dst"], [("dtr", g)], "st_dst", q="pool")

        s_load(0)
        norm_transpose(GX[0][0], GX[0][1], 0)
        ffn_gu_a("w1g", "w1u")
        ew32 = [ph.enter_context(nc.sbuf_tensor(f"sb_ew32_{i}", [128, 1024], F32)) for i in range(3)]
        ew16 = [ph.enter_context(nc.sbuf_tensor(f"sb_ew16_{i}", [128, 1024], BF16)) for i in range(3)]
        ek = 0
        for wname in ("w1d", "win"):
            Kw, Nw = w_src[wname].shape
            foldw = V_GMIX if wname == "win" else None
            for kc in range(Kw // 128):
                for c0 in range(0, Nw, 1024):
                    c1 = min(Nw, c0 + 1024)
                    sl = ek % 3
                    ek += 1
                    a32, a16, nn = ew32[sl], ew16[sl], c1 - c0
                    P.dma(a32[:, 0:nn], w_src[wname][kc * 128:(kc + 1) * 128, c0:c1], [], [f"ew32_{sl}"], f"ld_ew{sl}")
                    eng = "dve" if sl % 2 == 0 else "act"
                    if foldw is None:
                        if eng == "act":
                            P.op("act", lambda e, a32=a32, a16=a16, nn=nn: e.copy(out=a16[:, 0:nn], in_=a32[:, 0:nn]),
                                 [f"ew32_{sl}"], [f"ew16_{sl}"])
                        else:
                            P.op("dve", lambda e, a32=a32, a16=a16, nn=nn: e.tensor_copy(out=a16[:, 0:nn], in_=a32[:, 0:nn]),
                                 [f"ew32_{sl}"], [f"ew16_{sl}"])
                    else:
                        col = vecs[:, foldw + kc:foldw + kc + 1]
                        if eng == "act":
                            P.op("act", lambda e, a32=a32, a16=a16, nn=nn, col=col: e.activation(
                                out=a16[:, 0:nn], in_=a32[:, 0:nn], func=AF.Copy, scale=col), [f"ew32_{sl}", "vecs"], [f"ew16_{sl}"])
                        else:
                            P.op("dve", lambda e, a32=a32, a16=a16, nn=nn, col=col: e.tensor_scalar(
                                out=a16[:, 0:nn], in0=a32[:, 0:nn], scalar1=col, scalar2=None, op0=ALU.mult),
                                [f"ew32_{sl}", "vecs"], [f"ew16_{sl}"])
                    P.dma(wb[wname][:, kc, c0:c1], a16[:, 0:nn], [f"ew16_{sl}"], [("wb", wname)], f"st_ew{sl}", q="pool")
        ffn_down_a(GX[0][0], GX[0][1], "w1d")
        for g in range(ngroups):
            if g + 1 < ngroups:
                s_load(g + 1)
                norm_transpose(GX[g + 1][0], GX[g + 1][1], 0)
                ffn_gu_a("w1g", "w1u")
            s_rest(g)
            if g + 1 < ngroups:
                ffn_down_a(GX[g + 1][0], GX[g + 1][1], "w1d")
        late_step(len(late))
        P.barrier()
    if stop_after == "A":
        return finish(nc, es, P, sems, None)

    xs_tok = dscr("xs_tok", [NTOK, D], BF16)
    B_tok = dscr("B_tok", [NTOK, 512], BF16)
    BTs = dscr("BTs", [512, NTOK], BF16)
    CTs = dscr("CTs", [512, NTOK], BF16)
    yf_d = dscr("yf_d", [NTOK, D], F32)
    ssm_tok = dscr("ssm_tok", [NTOK, D], BF16)
    xbcT_v = xbcT.rearrange("(c p) t -> p c t", p=128)
    BTs_v = BTs.rearrange("(g p) t -> p g t", p=128)
    CTs_v = CTs.rearrange("(g p) t -> p g t", p=128)
    S0 = NPS + HALO
    own_groups = [(g * GT, "P", g, 4) for g in range(NPS // GT)] + [(S0 + g * GT, "S", g, TS // GT) for g in range(TS // GT)]
    with ExitStack() as ph:
        def psb(name, shape, dt):
            return ph.enter_context(nc.sbuf_tensor("sb_" + name, list(shape), dt))
        xcs = Rot([(psb(f"xc{i}", [128, 16, GT + 4], BF16), f"xc{i}") for i in range(2)])
        cvTs = Rot([(psb(f"cvT{i}", [128, 16, GT], BF16), f"cvT{i}") for i in range(2)])
        dg = psb("dg", [128, 5, 16, 128], BF16)
        for tap in range(5):
            for ch in range(16):
                eng = "dve" if (tap * 16 + ch) % 2 == 0 else "pool"
                wc = vecs[:, V_CW + tap * 16 + ch:V_CW + tap * 16 + ch + 1]
                P.op(eng, lambda e, tap=tap, ch=ch, wc=wc: e.tensor_scalar(out=dg[:, tap, ch, :], in0=ident_b, scalar1=wc, scalar2=None,
                                                                           op0=ALU.mult), ["cstb", "vecs"], [("dg", tap, ch)])
        xs_st = psb("xs_st", [128, 4, D], BF16)
        B_st = psb("B_st", [128, 4, 512], BF16)
        cv_banks = Rot([bk(4), bk(5), bk(6), bk(7)])
        tr_banks = Rot([bk(0), bk(1), bk(2), bk(3)])
        ev = Rot(["dve", "act"])
        for (tok0, seg, gi, ng) in own_groups:
            xc, xck = xcs.next()
            cvT, cvk = cvTs.next()
            lo, hi = tok0 - 2, tok0 + GT + 2
            c0, c1 = 0, GT + 4
            if seg == "P" and gi == 0:
                P.op("pool", lambda e, xc=xc: e.memset(xc[:, :, 0:2], 0.0), [], [xck])
                lo, c0 = tok0, 2
            if seg == "P" and gi == ng - 1:
                P.op("pool", lambda e, xc=xc: e.memset(xc[:, :, GT + 2:GT + 4], 0.0), [], [xck])
                hi, c1 = tok0 + GT, GT + 2
            P.dma(xc[:, :, c0:c1], xbcT_v[:, :, lo:hi], [("xbc", 16 + ch, tok0 // GT) for ch in range(16)], [xck], "ld_" + xck)
            for ch in range(16):
                pb, pbk = cv_banks.next()
                for tap in range(5):
                    P.op("pe", lambda e, tap=tap, ch=ch, pb=pb, xc=xc: e.matmul(
                        pb[:], lhsT=dg[:, tap, ch, :], rhs=xc[:, ch, tap:tap + GT], start=(tap == 0), stop=(tap == 4)),
                        [xck, ("dg", tap, ch)], [pbk])
                bcol = vecs[:, V_CB + ch:V_CB + ch + 1]
                P.op("act", lambda e, pb=pb, ch=ch, bcol=bcol, cvT=cvT: e.activation(out=cvT[:, ch, :], in_=pb[:], func=AF.Silu, bias=bcol),
                     [pbk, "vecs"], [(cvk, ch)])
            P.dma(BTs_v[:, :, tok0:tok0 + GT], cvT[:, 8:12, :], [(cvk, ch) for ch in range(8, 12)], [("BTs", tok0)], "st_bt", q="pool")
            P.dma(CTs_v[:, :, tok0:tok0 + GT], cvT[:, 12:16, :], [(cvk, ch) for ch in range(12, 16)], [("CTs", tok0)], "st_ct", q="pool")
            for t in range(4):
                bank, bkey = tr_banks.next()
                bv = bank[:].bitcast(BF16)
                for ch in range(8):
                    P.op("pe", lambda e, t=t, ch=ch, bv=bv, cvT=cvT: e.transpose(bv[:, ch * 128:(ch + 1) * 128],
                                                                        cvT[:, ch, t * 128:(t + 1) * 128], ident_b),
                         [(cvk, ch), "cstb"], [bkey])
                eng = ev.next()
                if eng == "act":
                    P.op("act", lambda e, t=t, bv=bv: e.copy(out=xs_st[:, t, :], in_=bv), [bkey], ["xs_st"])
                else:
                    P.op("dve", lambda e, t=t, bv=bv: e.tensor_copy(out=xs_st[:, t, :], in_=bv), [bkey], ["xs_st"])
                bank, bkey = tr_banks.next()
                bv = bank[:].bitcast(BF16)
                for ch in range(8, 12):
                    P.op("pe", lambda e, t=t, ch=ch, bv=bv, cvT=cvT: e.transpose(bv[:, (ch - 8) * 128:(ch - 7) * 128],
                                                                        cvT[:, ch, t * 128:(t + 1) * 128], ident_b),
                         [(cvk, ch), "cstb"], [bkey])
                eng = ev.next()
                if eng == "act":
                    P.op("act", lambda e, t=t, bv=bv: e.copy(out=B_st[:, t, :], in_=bv[:, 0:512]), [bkey], ["B_st"])
                else:
                    P.op("dve", lambda e, t=t, bv=bv: e.tensor_copy(out=B_st[:, t, :], in_=bv[:, 0:512]), [bkey], ["B_st"])
            P.dma(xs_tok[tok0:tok0 + GT, :].rearrange("(t p) c -> p t c", p=128), xs_st[:], ["xs_st"], [("xs_tok", tok0)], "st_xs", q="pool")
            P.dma(B_tok[tok0:tok0 + GT, :].rearrange("(t p) c -> p t c", p=128), B_st[:], ["B_st"], [("B_tok", tok0)], "st_bs", q="pool")
        P.barrier()
    if stop_after == "B1":
        return finish(nc, es, P, sems, None)

    sel_d = din("sel", [128, 4])
    ccin = [nc.dram_tensor(f"ccin{i}", [128, D if i < 2 else 32], F32).ap() for i in range(3)]
    ccout = [nc.dram_tensor(f"ccout{i}", [512, D if i < 2 else 32], F32).ap() for i in range(3)]
    with ExitStack() as ph:
        def psb(name, shape, dt):
            return ph.enter_context(nc.sbuf_tensor("sb_" + name, list(shape), dt))
        ident_f = cst[:, C_ID:C_ID + 128]
        ones_f = cst[:, C_ONES:C_ONES + 128]
        tri = [cst[:, C_TRIF:C_TRIF + 128], cst[:, C_TRIB:C_TRIB + 128]]
        A_t = psb("A_t", [128, 32], F32)
        P.op("act", lambda e: e.activation(out=A_t[:], in_=vecs[:, V_ALOG:V_ALOG + 32], func=AF.Exp), ["vecs"], ["A_t"])
        P.op("dve", lambda e: e.tensor_scalar(out=A_t[:], in0=A_t[:], scalar1=-1.0, scalar2=None, op0=ALU.mult), ["A_t"], ["A_t"])
        sel = psb("sel", [128, 4], F32)
        P.dma(sel[:], sel_d, [], ["sel"], "ld_sel")
        xsg = Rot([(psb(f"xsg{i}", [128, 4, D], BF16), f"xsg{i}") for i in range(2)])
        Bg = Rot([(psb(f"Bg{i}", [128, 4, 512], BF16), f"Bg{i}") for i in range(2)])
        BTg = Rot([(psb(f"BTg{i}", [128, 4, GT], BF16), f"BTg{i}") for i in range(2)])
        CTg = Rot([(psb(f"CTg{i}", [128, 4, GT], BF16), f"CTg{i}") for i in range(2)])
        dtg = Rot([(psb(f"dtg{i}", [128, 4, 32], F32), f"dtg{i}") for i in range(2)])
        yfg = Rot([(psb(f"yfg{i}", [128, 4, D], F32), f"yfg{i}") for i in range(2)])
        zg = Rot([(psb(f"zg{i}", [128, 4, D], BF16), f"zg{i}") for i in range(2)])
        ssm_st = psb("ssm_st", [128, 4, D], BF16)
        pres = Rot([(psb(f"pre{i}", [128, 48], F32), f"pre{i}") for i in range(2)])
        exs = Rot([(psb(f"ex{i}", [128, 48], F32), f"ex{i}") for i in range(2)])
        nacs = Rot([(psb(f"nac{i}", [128, 16], F32), f"nac{i}") for i in range(2)])
        dts = Rot([(psb(f"dt{i}", [128, 16], F32), f"dt{i}") for i in range(2)])
        avs = Rot([(psb(f"av{i}", [128, 16], F32), f"av{i}") for i in range(2)])
        dtxs = Rot([(psb(f"dtx{i}", [128, D], BF16), f"dtx{i}") for i in range(2)])
        wdtxs = Rot([(psb(f"wdtx{i}", [128, D], BF16), f"wdtx{i}") for i in range(2)])
        cbms = Rot([(psb(f"cbm{i}", [128, 4, 128], BF16), f"cbm{i}") for i in range(2)])
        segs = Rot([(psb(f"seg{i}", [128, 4, 128], F32), f"seg{i}") for i in range(2)])
        lexs = Rot([(psb(f"lex{i}", [128, 4, 128], BF16), f"lex{i}") for i in range(2)])
        Ms = Rot([(psb(f"M{i}", [128, 4, 128], BF16), f"M{i}") for i in range(3)])
        tmps = Rot([(psb(f"tmp{i}", [128, 512], F32), f"tmp{i}") for i in range(2)])
        sig = psb("sig", [128, D], F32)
        yaccs = Rot([(psb(f"yacc{i}", [128, D], F32), f"yacc{i}") for i in range(2)])
        gss = psb("gss", [128, 8], F32)
        junk2 = psb("junk2", [128, 256], F32)
        H = [psb(f"H{d}", [128, D], F32) for d in range(2)]
        Hbf = [psb(f"Hbf{d}", [128, D], BF16) for d in range(2)]
        Hin = [psb(f"Hin{d}", [128, D], F32) for d in range(2)]
        Dacc = psb("Dacc", [128, 32], F32)
        bA, bC = bk(0), bk(1)
        bL = Rot([bk(2), bk(3)])
        bY = Rot([bk(4), bk(5)])
        bO = bk(6)
        bS = Rot([bk(7)])

        def h3(ap):
            return ap.rearrange("p (h d) -> p h d", d=64)

        def bc3(ap16, n=64):
            return ap16.unsqueeze(2).to_broadcast([128, ap16.shape[1], n])

        def load_group(tok0, full, need_y):
            a, ak = xsg.next(); b, bkk = Bg.next(); dtt, dtk = dtg.next()
            rr = lambda ap: ap[tok0:tok0 + GT, :].rearrange("(t p) c -> p t c", p=128)
            P.dma(a[:], rr(xs_tok), [("xs_tok", tok0)], [ak], "ld_" + ak)
            P.dma(b[:], rr(B_tok), [("B_tok", tok0)], [bkk], "ld_" + bkk)
            P.dma(dtt[:], rr(dtr), [("dtr", tok0 // GT)], [dtk], "ld_" + dtk)
            res = dict(xs=(a, ak), B=(b, bkk), dt=(dtt, dtk))
            if full:
                bt, btk = BTg.next(); ct, ctk = CTg.next()
                P.dma(bt[:], BTs_v[:, :, tok0:tok0 + GT], [("BTs", tok0)], [btk], "ld_" + btk)
                P.dma(ct[:], CTs_v[:, :, tok0:tok0 + GT], [("CTs", tok0)], [ctk], "ld_" + ctk)
                res["BT"] = (bt, btk); res["CT"] = (ct, ctk)
            if need_y:
                yy, yk = yfg.next(); zz, zk = zg.next()
                P.dma(yy[:], rr(yf_d), [("yf", tok0)], [yk], "ld_" + yk)
                P.dma(zz[:], rr(ztok), [("ztok", tok0 // GT)], [zk], "ld_" + zk)
                res["yf"] = (yy, yk); res["z"] = (zz, zk)
            return res

        def ssd_chunk(G, t, d, full):
            xs_t, xsk = G["xs"]; B_t, Bk = G["B"]; dt_t, dtk = G["dt"]
            pre, prek = pres.next(); ex, exk = exs.next(); nac, nack = nacs.next()
            dtv, dtvk = dts.next(); av, avk = avs.next()
            dtx, dtxk = dtxs.next(); wdtx, wdtxk = wdtxs.next()
            Hd, Hk, Hb, Hbk = H[d], f"H{d}", Hbf[d], f"Hbf{d}"
            P.op("dve", lambda e: e.tensor_tensor(out=dtv[:], in0=dt_t[:, t, d * 16:(d + 1) * 16],
                                                  in1=vecs[:, V_DTB + d * 16:V_DTB + (d + 1) * 16], op=ALU.add),
                 [dtk, "vecs"], [dtvk])
            P.op("act", lambda e: e.activation(out=dtv[:], in_=dtv[:], func=AF.Exp), [dtvk], [dtvk])
            P.op("act", lambda e: e.activation(out=dtv[:], in_=dtv[:], func=AF.Ln, bias=1.0), [dtvk], [dtvk])
            P.op("dve", lambda e: e.tensor_tensor(out=av[:], in0=dtv[:], in1=A_t[:, d * 16:(d + 1) * 16], op=ALU.mult),
                 [dtvk, "A_t"], [avk])
            pa, pak = bA
            P.op("pe", lambda e: e.matmul(pa[:, 0:16], lhsT=tri[d], rhs=av[:], start=True, stop=True), [avk, "cst"], [pak])
            P.op("pe", lambda e: e.matmul(pa[:, 16:32], lhsT=ones_f, rhs=av[:], start=True, stop=True), [avk, "cst"], [pak])
            P.op("dve", lambda e: e.tensor_copy(out=pre[:, 0:32], in_=pa[:, 0:32]), [pak], [prek])
            P.op("dve", lambda e: e.tensor_tensor(out=pre[:, 32:48], in0=pre[:, 16:32], in1=pre[:, 0:16], op=ALU.subtract),
                 [prek], [prek])
            P.op("act", lambda e: e.activation(out=ex[:], in_=pre[:], func=AF.Exp), [prek], [exk])
            P.op("pool" if full else "dve", lambda e: e.tensor_tensor(out=h3(dtx[:]), in0=h3(xs_t[:, t, :]), in1=bc3(dtv[:, 0:16]), op=ALU.mult),
                 [xsk, dtvk], [dtxk])
            P.op("pool", lambda e: e.tensor_tensor(out=h3(wdtx[:]), in0=h3(dtx[:]), in1=bc3(ex[:, 32:48]), op=ALU.mult),
                 [dtxk, exk], [wdtxk])
            if full:
                BT_t, BTk = G["BT"]; CT_t, CTk = G["CT"]
                P.op("dve", lambda e: e.tensor_scalar(out=nac[:], in0=pre[:, 0:16], scalar1=-1.0, scalar2=None, op0=ALU.mult),
                     [prek], [nack])
                pc, pck = bC
                for g in range(4):
                    P.op("pe", lambda e, g=g: e.matmul(pc[:, g * 128:(g + 1) * 128], lhsT=BT_t[:, g, t * 128:(t + 1) * 128],
                                                       rhs=CT_t[:, g, t * 128:(t + 1) * 128], start=True, stop=True),
                         [BTk, CTk], [pck])
                cbm, cbmk = cbms.next()
                P.op("dve", lambda e: e.tensor_tensor(out=cbm[:], in0=pc[:].rearrange("p (g l) -> p g l", g=4),
                                                      in1=tri[d].unsqueeze(1).to_broadcast([128, 4, 128]), op=ALU.mult),
                     [pck, "cst"], [cbmk])

            def main(ystage, hook=None):
                if full:
                    BT_t, BTk = G["BT"]; CT_t, CTk = G["CT"]
                    main_full(ystage, BT_t, BTk, CT_t, CTk, hook)
                return main_tail()

            def main_full(ystage, BT_t, BTk, CT_t, CTk, hook=None):
                    pl_of = {}

                    def emit_T(g):
                        pl, plk = bL.next()
                        for j in range(4):
                            h = g * 4 + j
                            P.op("pe", lambda e, j=j, h=h, pl=pl: e.transpose(pl[:, j * 128:(j + 1) * 128],
                                                                              pre[:, h:h + 1].to_broadcast([128, 128]), ident_f),
                                 [prek, "cst"], [plk])
                        pl_of[g] = (pl, plk)

                    emit_T(0)
                    for hh in range(2):
                        py, pyk = bY.next()
                        po, pok = bO
                        for gg in range(2):
                            g = hh * 2 + gg
                            if g + 1 < 4:
                                emit_T(g + 1)
                            pl, plk = pl_of[g]
                            lx, lxk = lexs.next(); M, Mk = Ms.next()
                            for j in range(4):
                                h = g * 4 + j
                                P.op("act", lambda e, j=j, h=h, pl=pl, lx=lx: e.activation(
                                    out=lx[:, j, :], in_=pl[:, j * 128:(j + 1) * 128], func=AF.Exp, bias=nac[:, h:h + 1]),
                                    [plk, nack], [(lxk, j)])
                            P.op("dve", lambda e, lx=lx, M=M, g=g: e.scalar_tensor_tensor(
                                out=M[:], in0=lx[:], scalar=1.0, in1=cbm[:, g:g + 1, :].to_broadcast([128, 4, 128]),
                                op0=ALU.min, op1=ALU.mult), [(lxk, j) for j in range(4)] + [cbmk], [Mk])
                            for j in range(4):
                                h = g * 4 + j
                                c0 = (h % 8) * 64
                                P.op("pe", lambda e, j=j, h=h, c0=c0, M=M, py=py: e.matmul(
                                    py[:, c0:c0 + 64], lhsT=M[:, j, :], rhs=dtx[:, h * 64:(h + 1) * 64], start=True, stop=True),
                                    [Mk, dtxk], [pyk])
                            P.op("pe", lambda e, g=g, gg=gg, po=po: e.matmul(
                                po[:, gg * 256:(gg + 1) * 256], lhsT=CT_t[:, g, t * 128:(t + 1) * 128],
                                rhs=Hb[:, g * 256:(g + 1) * 256], start=True, stop=True), [CTk, Hbk], [pok])
                            if hook is not None:
                                hook()
                        tm, tmk = tmps.next()
                        P.op("dve", lambda e, hh=hh, po=po, tm=tm: e.tensor_tensor(
                            out=tm[:].rearrange("p (h d) -> p h d", d=64), in0=po[:].rearrange("p (h d) -> p h d", d=64),
                            in1=bc3(ex[:, hh * 8:(hh + 1) * 8]), op=ALU.mult), [pok, exk], [tmk])
                        P.op("dve", lambda e, hh=hh, py=py, tm=tm: e.tensor_tensor(
                            out=ystage[0][:, hh * 512:(hh + 1) * 512], in0=tm[:], in1=py[:], op=ALU.add), [pyk, tmk], [ystage[1]])
                        if hook is not None:
                            hook()

            def main_tail():
                for hh in range(2):
                    ps_, psk = bS.next()
                    for gg in range(2):
                        g = hh * 2 + gg
                        P.op("pe", lambda e, g=g, gg=gg, ps_=ps_: e.matmul(
                            ps_[:, gg * 256:(gg + 1) * 256], lhsT=B_t[:, t, g * 128:(g + 1) * 128],
                            rhs=wdtx[:, g * 256:(g + 1) * 256], start=True, stop=True), [Bk, wdtxk], [psk])
                    sl = slice(hh * 512, (hh + 1) * 512)
                    P.op("pool" if full else "dve", lambda e, hh=hh, sl=sl: e.tensor_tensor(
                        out=Hd[:, sl].rearrange("p (h d) -> p h d", d=64), in0=Hd[:, sl].rearrange("p (h d) -> p h d", d=64),
                        in1=bc3(ex[:, 16 + hh * 8:16 + (hh + 1) * 8]), op=ALU.mult), [Hk, exk], [Hk])
                    P.op("dve", lambda e, sl=sl, ps_=ps_: e.tensor_tensor(out=Hd[:, sl], in0=Hd[:, sl], in1=ps_[:], op=ALU.add),
                         [Hk, psk], [Hk])
                if full:
                    P.op("act", lambda e: e.copy(out=Hb[:], in_=Hd[:]), [Hk], [Hbk])
                return ex, exk

            return main

        def init_state(d, src=None):
            if src is None:
                P.op("pool", lambda e: e.memset(H[d][:], 0.0), [], [f"H{d}"])
            else:
                P.op("pool", lambda e: e.tensor_copy(out=H[d][:], in_=src[0][:]), [src[1]], [f"H{d}"])
            P.op("act", lambda e: e.copy(out=Hbf[d][:], in_=H[d][:]), [f"H{d}"], [f"Hbf{d}"])

        def sweep(base, nchunks, d, full, light_acc=False):
            groups = list(range(nchunks // 4))
            if d == 1:
                groups = groups[::-1]
            ts = [0, 1, 2, 3] if d == 0 else [3, 2, 1, 0]
            chunks = [(gi, t) for gi in groups for t in ts]
            Gs = {}

            def get_group(gi):
                if gi not in Gs:
                    Gs[gi] = load_group(base + gi * GT, full, need_y=(full and d == 1))
                return Gs[gi]

            def do_pre(i):
                gi, t = chunks[i]
                return ssd_chunk(get_group(gi), t, d, full)

            deferred = []

            def run_deferred():
                if deferred:
                    deferred.pop(0)()

            nxt = do_pre(0)
            for i, (gi, t) in enumerate(chunks):
                main = nxt
                G = get_group(gi)
                tok0 = base + gi * GT
                if i + 1 < len(chunks):
                    nxt = do_pre(i + 1)
                if full and d == 0:
                    if "ystage" not in G:
                        G["ystage"] = yfg.next()
                    yy, yk = G["ystage"]
                    ex, exk = main((yy[:, t, :], yk))
                elif full:
                    yacc, yak = yaccs.next()
                    ex, exk = main((yacc[:], yak), hook=run_deferred)
                    yy, yk = G["yf"]; zz, zk = G["z"]; xs_t, xsk = G["xs"]

                    def part1(yacc=yacc, yak=yak, yy=yy, yk=yk, xs_t=xs_t, xsk=xsk, t=t):
                        P.op("dve", lambda e: e.tensor_tensor(out=yacc[:], in0=yacc[:], in1=yy[:, t, :], op=ALU.add), [yak, yk], [yak])
                        P.op("pool", lambda e: e.tensor_tensor(out=h3(sig[:]), in0=h3(xs_t[:, t, :]), in1=bc3(vecs[:, V_DSK:V_DSK + 16]),
                                                               op=ALU.mult), [xsk, "vecs"], ["sig"])

                    def part2(yacc=yacc, yak=yak, zz=zz, zk=zk, t=t):
                        P.op("dve", lambda e: e.tensor_tensor(out=yacc[:], in0=yacc[:], in1=sig[:], op=ALU.add), [yak, "sig"], [yak])
                        P.op("act", lambda e: e.activation(out=sig[:], in_=zz[:, t, :], func=AF.Silu), [zk, "sig"], ["sig"])

                    def part3(yacc=yacc, yak=yak):
                        P.op("dve", lambda e: e.tensor_tensor(out=yacc[:], in0=yacc[:], in1=sig[:], op=ALU.mult), [yak, "sig"], [yak])
                        for g in range(4):
                            P.op("act", lambda e, g=g: e.activation(out=junk2[:], in_=yacc[:, g * 256:(g + 1) * 256], func=AF.Square,
                                                                    accum_out=gss[:, g:g + 1]), [yak], [("gss", g)])

                    def part4(yacc=yacc, yak=yak, t=t):
                        P.op("act", lambda e: e.activation(out=gss[:, 4:8], in_=gss[:, 0:4], func=AF.Ln, scale=1.0 / 256, bias=EPS),
                             [("gss", g) for g in range(4)], ["gssr"])
                        P.op("act", lambda e: e.activation(out=gss[:, 4:8], in_=gss[:, 4:8], func=AF.Exp, scale=-0.5), ["gssr"], ["gssr"])
                        for g in range(4):
                            P.op("dve", lambda e, g=g: e.tensor_scalar(
                                out=ssm_st[:, t, g * 256:(g + 1) * 256], in0=yacc[:, g * 256:(g + 1) * 256],
                                scalar1=gss[:, 4 + g:5 + g], scalar2=None, op0=ALU.mult), [yak, "gssr"], ["ssm_st"])

                    deferred.extend([part1, part2, part3, part4])
                    if t == ts[-1]:
                        def store_part(tok0=tok0):
                            P.dma(ssm_tok[tok0:tok0 + GT, :].rearrange("(t p) c -> p t c", p=128), ssm_st[:], ["ssm_st"],
                                  [("ssm_tok", tok0)], "st_ssm", q="pool")
                        deferred.append(store_part)
                else:
                    ex, exk = main(None)
                if light_acc:
                    P.op("dve", lambda e, ex=ex: e.tensor_tensor(out=Dacc[:, d * 16:(d + 1) * 16], in0=Dacc[:, d * 16:(d + 1) * 16],
                                                                 in1=ex[:, 16:32], op=ALU.mult), [exk, "Dacc"], ["Dacc"])
                last_in_group = (t == ts[-1])
                if last_in_group and full and d == 0:
                    yy, yk = G["ystage"]
                    P.dma(yf_d[tok0:tok0 + GT, :].rearrange("(t p) c -> p t c", p=128), yy[:], [yk], [("yf", tok0)], "st_" + yk, q="pool")

            while deferred:
                deferred.pop(0)()

        P.op("pool", lambda e: e.memset(Dacc[:], 1.0), [], ["Dacc"])
        for d in range(2):
            init_state(d)
            sweep(S0, TS // 128, d, False, light_acc=True)
            P.dma(ccin[d], H[d][:], [f"H{d}"], [f"ccin{d}"], f"st_cc{d}", q="pool")
        P.dma(ccin[2], Dacc[:], ["Dacc"], ["ccin2"], "st_cc2", q="pool")
        for i in range(3):
            P.op("pool", lambda e, i=i: e.collective_compute("AllGather", ALU.bypass, replica_groups=[[0, 1, 2, 3], [4, 5, 6, 7]],
                                                             ins=[ccin[i].opt()], outs=[ccout[i].opt()]),
                 [f"ccin{i}"], [f"ccout{i}"], chan=f"cc{i}", inc=1)
        for d in range(2):
            init_state(d)
            sweep(0, NPS // 128, d, True)
        Ff = Rot([(psb(f"Ff{i}", [128, D], F32), f"Ff{i}") for i in range(2)])
        Dd = Rot([(psb(f"Dd{i}", [128, 32], F32), f"Dd{i}") for i in range(2)])
        Gc = psb("Gc", [128, D], F32)
        for d in range(2):
            P.op("pool", lambda e: e.memset(Gc[:], 0.0), [], ["Gc"])
            P.op("pool", lambda e, d=d: e.memset(Hin[d][:], 0.0), [], [f"Hin{d}"])
            order = [0, 1, 2, 3] if d == 0 else [3, 2, 1, 0]
            for i in order:
                ff, ffk = Ff.next(); dd, ddk = Dd.next()
                P.dma(ff[:], ccout[d][i * 128:(i + 1) * 128, :], [f"ccout{d}"], [ffk], "ld_" + ffk)
                P.dma(dd[:], ccout[2][i * 128:(i + 1) * 128, :], ["ccout2"], [ddk], "ld_" + ddk)
                P.op("dve", lambda e, d=d, i=i: e.scalar_tensor_tensor(out=Hin[d][:], in0=Gc[:], scalar=sel[:, i:i + 1], in1=Hin[d][:],
                                                                       op0=ALU.mult, op1=ALU.add), ["Gc", "sel", f"Hin{d}"], [f"Hin{d}"])
                P.op("dve", lambda e, d=d, dd=dd: e.tensor_tensor(out=h3(Gc[:]), in0=h3(Gc[:]), in1=bc3(dd[:, d * 16:(d + 1) * 16]),
                                                                  op=ALU.mult), ["Gc", ddk], ["Gc"])
                P.op("dve", lambda e, ff=ff: e.tensor_tensor(out=Gc[:], in0=Gc[:], in1=ff[:], op=ALU.add), ["Gc", ffk], ["Gc"])
        for d in range(2):
            init_state(d, (Hin[d], f"Hin{d}"))
            sweep(S0, TS // 128, d, True)
        P.barrier()
    if stop_after == "B2":
        return finish(nc, es, P, sems, None)

    P.mute = False
    relb_d = din("relb", [33, 16])
    onehot_d = din("onehot", [33, 3 * 384])
    kval_d = din("kval", [128, NKV])
    tvd = nc.dram_tensor("tvd", [16, 3 * 384], F32).ap()
    attnT = dscr("attnT", [D, NTOK], BF16)
    with ExitStack() as ph:
        def psb(name, shape, dt):
            return ph.enter_context(nc.sbuf_tensor("sb_" + name, list(shape), dt))
        anti_f = cst[:, C_ANTI:C_ANTI + 128]
        ones_f = cst[:, C_ONES:C_ONES + 128]
        relb = psb("relb", [33, 16], F32)
        oneh = psb("oneh", [33, 3 * 384], F32)
        kval = psb("kval", [128, NKV], F32)
        tvs = psb("tvs", [16, 3 * 384], F32)
        EB = psb("EB", [128, 3, 4, 2, 2, 2, 128], BF16)
        hks = Rot([(psb(f"hk{i}", [128, 2, 128], F32), f"hk{i}") for i in range(8)])
        P.dma(relb[:], relb_d, [], ["relb"], "ld_relb")
        P.dma(oneh[:], onehot_d, [], ["oneh"], "ld_oneh")
        P.dma(kval[:], kval_d, [], ["kval"], "ld_kval")
        for p in range(3):
            pb, pbk = bk(p)
            P.op("pe", lambda e, p=p, pb=pb: e.matmul(pb[0:16, 0:384], lhsT=relb[:], rhs=oneh[:, p * 384:(p + 1) * 384],
                                                     start=True, stop=True), ["relb", "oneh"], [pbk])
            P.op("dve", lambda e, p=p, pb=pb: e.tensor_copy(out=tvs[:, p * 384:(p + 1) * 384], in_=pb[0:16, 0:384]), [pbk], ["tvs"])
        P.dma(tvd, tvs[:], ["tvs"], ["tvd"], "st_tvs", q="pool")
        eb_banks = Rot([bk(4), bk(5), bk(6), bk(7)])
        for p in range(3):
            for h4 in range(4):
                hk_of = {}
                for hl in range(4):
                    h = h4 * 4 + hl
                    hk, hkk = hks.next()
                    src = bass.AP(tvd.tensor, h * 1152 + p * 384, [[1, 128], [128, 2], [1, 128]])
                    P.dma(hk[:], src, ["tvd"], [hkk], "ld_" + hkk)
                    hk_of[hl] = (hk, hkk)
                for kt in range(2):
                    pb, pbk = eb_banks.next()
                    for hl in range(4):
                        hk, hkk = hk_of[hl]
                        P.op("pe", lambda e, hk=hk, kt=kt, hl=hl, pb=pb: e.matmul(
                            pb[:, hl * 128:(hl + 1) * 128], lhsT=hk[:, kt, :], rhs=anti_f, start=True, stop=True),
                            [hkk, "cst"], [pbk])
                    P.op("act", lambda e, p=p, kt=kt, h4=h4, pb=pb: e.activation(
                        out=EB[:, p, h4, :, kt, :, :], in_=pb[:].rearrange("p (c e q) -> p e c q", c=2, e=2), func=AF.Copy, scale=8.0),
                        [pbk], [("EB", p, h4)])
        P.barrier()
        Vb = Rot([(psb(f"Vb{i}", [128, 17, 4, 65], BF16), f"Vb{i}") for i in range(4)])
        qTs = psb("qTs", [128, 2, TS], BF16)
        kTs = psb("kTs", [128, 2, TS + 2 * HALO], BF16)
        accT = psb("accT", [128, 4, TS], F32)
        pts = Rot([(psb(f"pt{i}", [128, 2, 2, 2, 128], BF16), f"pt{i}") for i in range(4)])
        obs = Rot([(psb(f"ob{i}", [64, 4, 512], BF16), f"ob{i}") for i in range(2)])
        sc_banks = Rot([(bk(0), bk(1)), (bk(2), bk(3)), (bk(4), bk(5))])
        ot_banks = Rot([bk(6), bk(7)])
        bc_banks = ot_banks
        LOOK = 2
        kvcol = 0
        norm_q = []

        def norm_prefetch():
            if norm_q and not norm_q[0]["done1"]:
                norm_q[0]["s1"]()
                norm_q[0]["done1"] = True

        def norm_pop():
            it = norm_q.pop(0)
            if not it["done1"]:
                it["s1"]()
            it["s2"]()
            norm_prefetch()
        for (seg, T, own_abs) in (("P", NPS, 0), ("S", TS, NPS + HALO)):
            kv0 = kvcol
            for hq in range(4):
                kvcol = kv0
                rows = lambda ap, hq=hq: ap[hq * 256:(hq + 1) * 256, :].rearrange("(c p) t -> p c t", p=128)
                P.dma(qTs[:, :, 0:T], rows(qT)[:, :, own_abs:own_abs + T], [], ["qTs"], "ld_qTs")
                P.dma(kTs[:, :, 0:T + 2048], rows(kT_full)[:, :, own_abs:own_abs + T + 2048], [], ["kTs"], "ld_kTs")
                blocks = []
                for p, dil in enumerate((1, 4, 16)):
                    nblk = T // (128 * dil)
                    for r in range(dil):
                        for ub in range(0, nblk, 16):
                            nb = min(16, nblk - ub)
                            unit = dict(ub=ub, nb=nb, buf=None)
                            for b in range(ub, ub + nb):
                                blocks.append(dict(p=p, dil=dil, r=r, b=b, nblk=nblk, unit=unit, first=(p == 0), kvc=kvcol))
                        kvcol += nblk + 1

                def stageA(B):
                    p, dil, r, b, nblk, unit = B["p"], B["dil"], B["r"], B["b"], B["nblk"], B["unit"]
                    if unit["buf"] is None:
                        vb, vbk = Vb.next()
                        unit["buf"] = (vb, vbk)
                        row0 = KPAD + own_abs + r - 64 * dil + 128 * dil * unit["ub"]
                        ntile = unit["nb"] + 1
                        for j0 in range(0, ntile, 9):
                            nj = min(9, ntile - j0)
                            src = bass.AP(vtok_full.tensor, (row0 + 128 * dil * j0) * VW + hq * 260,
                                          [[dil * VW, 128], [128 * dil * VW, nj], [1, 260]])
                            P.dma(vb[:, j0:j0 + nj, :, :].rearrange("p j h d -> p j (h d)"), src, [], [vbk], "ld_" + vbk)
                    (sx, sxk), (sy, syk) = sc_banks.next()
                    sbank = ((sx, sxk), (sy, syk))
                    q0 = r + dil * 128 * b
                    qsl = slice(q0, q0 + 127 * dil + 1, dil)
                    B["qsl"] = qsl
                    for e2 in range(2):
                        sb_, sbk_ = sbank[e2]
                        P.op("pe", lambda e, e2=e2, sb_=sb_, p=p, hq=hq: e.matmul(
                            sb_[:], lhsT=ident_b, rhs=EB[:, p, hq, e2, :, :, :].rearrange("p k c q -> p (k c q)"),
                            start=True, stop=False), ["cstb", ("EB", p, hq)], [sbk_])
                    for kt in range(2):
                        u0 = 1024 + r + dil * (128 * b - 64 + 128 * kt)
                        ksl = slice(u0, u0 + 127 * dil + 1, dil)
                        for c in range(2):
                            for e2 in range(2):
                                sb_, sbk_ = sbank[e2]
                                o0 = (kt * 2 + c) * 128
                                P.op("pe", lambda e, c=c, e2=e2, o0=o0, ksl=ksl, qsl=qsl, sb_=sb_, last=(kt == 1 and c == 1): e.matmul(
                                    sb_[:, o0:o0 + 128], lhsT=kTs[64 * e2:64 * e2 + 64, c, ksl],
                                    rhs=qTs[64 * e2:64 * e2 + 64, c, qsl], start=False, stop=last), ["qTs", "kTs"], [sbk_])
                    pt, ptk = pts.next()
                    B["pt"] = (pt, ptk)
                    interior = (b >= 1) and (b + 1 <= nblk - 1)
                    for e2 in range(2):
                        sb_, sbk_ = sbank[e2]
                        if interior:
                            P.op("act", lambda e, e2=e2, sb_=sb_, pt=pt: e.activation(
                                out=pt[:, e2, :, :, :].rearrange("p k c q -> p (k c q)"), in_=sb_[:], func=AF.Exp, scale=0.125),
                                [sbk_], [(ptk, e2)])
                        else:
                            for kt in range(2):
                                col = B["kvc"] + b + kt
                                P.op("act", lambda e, e2=e2, kt=kt, sb_=sb_, pt=pt, col=col: e.activation(
                                    out=pt[:, e2, kt, :, :].rearrange("p c q -> p (c q)"), in_=sb_[:, kt * 256:(kt + 1) * 256],
                                    func=AF.Exp, bias=kval[:, col:col + 1], scale=0.125), [sbk_, "kval"], [(ptk, e2)])

                def stageB(B):
                    b, unit = B["b"], B["unit"]
                    vb, vbk = unit["buf"]
                    pt, ptk = B["pt"]
                    qsl = B["qsl"]
                    ot, otk = ot_banks.next()
                    for c in range(2):
                        for e2 in range(2):
                            hl = 2 * c + e2
                            for kt in range(2):
                                jt = b - unit["ub"] + kt
                                P.op("pe", lambda e, hl=hl, c=c, e2=e2, kt=kt, jt=jt: e.matmul(
                                    ot[0:65, hl * 128:(hl + 1) * 128], lhsT=vb[:, jt, hl, :], rhs=pt[:, e2, kt, c, :],
                                    start=(kt == 0), stop=(kt == 1)), [vbk, (ptk, 0), (ptk, 1)], [otk])
                    acc_v = accT[0:65, :, qsl]
                    otv = ot[0:65, :].rearrange("p (h q) -> p h q", h=4)
                    akeys = [("accT", cbi) for cbi in range(qsl.start // 512, (qsl.stop - 1) // 512 + 1)]
                    if B["first"]:
                        P.op("dve", lambda e: e.tensor_copy(out=acc_v, in_=otv), [otk], akeys)
                    else:
                        P.op("dve", lambda e: e.tensor_tensor(out=acc_v, in0=acc_v, in1=otv, op=ALU.add), [otk] + akeys, akeys)

                nB = len(blocks)
                norm_prefetch()
                for i in range(nB + LOOK):
                    if i < nB:
                        stageA(blocks[i])
                    if i - LOOK >= 0:
                        Bk = blocks[i - LOOK]
                        if Bk["p"] == 0 and Bk["b"] % 4 == 0 and norm_q:
                            norm_pop()
                        stageB(Bk)
                while norm_q:
                    norm_pop()
                for cb in range(T // 512):
                    def norm_s1(cb=cb):
                        P.op("act", lambda e: e.activation(out=accT[64:65, :, cb * 512:(cb + 1) * 512],
                                                           in_=accT[64:65, :, cb * 512:(cb + 1) * 512], func=AF.Ln),
                             [("accT", cb)], [("accT", cb)])
                        P.op("act", lambda e: e.activation(out=accT[64:65, :, cb * 512:(cb + 1) * 512],
                                                           in_=accT[64:65, :, cb * 512:(cb + 1) * 512], func=AF.Exp, scale=-1.0),
                             [("accT", cb)], [("accT", cb)])

                    def norm_s2(cb=cb, hq=hq, own_abs=own_abs):
                        ob, obk = obs.next()
                        for hl in range(4):
                            pb, pbk = bc_banks.next()
                            P.op("pe", lambda e, hl=hl, pb=pb: e.matmul(
                                pb[0:64, :], lhsT=ones_f[64:65, 0:64], rhs=accT[64:65, hl, cb * 512:(cb + 1) * 512], start=True, stop=True),
                                [("accT", cb), "cst"], [pbk])
                            P.op("dve", lambda e, hl=hl, pb=pb, ob=ob: e.tensor_tensor(
                                out=ob[:, hl, :], in0=accT[0:64, hl, cb * 512:(cb + 1) * 512], in1=pb[0:64, :], op=ALU.mult),
                                [pbk, ("accT", cb)], [obk])
                        c0 = own_abs + cb * 512
                        P.dma(attnT[hq * 256:(hq + 1) * 256, c0:c0 + 512].rearrange("(hl p) t -> p hl t", p=64), ob[:], [obk],
                              [("attnT", hq, c0)], "st_" + obk, q="pool")
                    norm_q.append({"s1": norm_s1, "s2": norm_s2, "done1": False})
        while norm_q:
            norm_pop()
        assert kvcol == NKV, kvcol
        P.barrier()
    if stop_after == "B3":
        return finish(nc, es, P, sems, None)

    with ExitStack() as ph:
        C = ffn_ctx(ph, "c")
        def psb(name, shape, dt):
            return ph.enter_context(nc.sbuf_tensor("sb_c2" + name, list(shape), dt))
        ones_b = cstb[:, C_ONES:C_ONES + 128]
        atg = Rot([(psb(f"atg{i}", [128, 8, GT], BF16), f"atg{i}") for i in range(2)])
        smg = Rot([(psb(f"smg{i}", [128, 4, D], BF16), f"smg{i}") for i in range(2)])
        smT = psb("smT", [128, 8, GT], BF16)
        asq = psb("asq", [128, 8, GT], BF16)
        ars = psb("ars", [128, 8], F32)
        tr_banks = Rot([bk(0), bk(1)])
        oa_banks = Rot([bk(2), bk(3)])
        os_banks = Rot([bk(4), bk(5)])
        for (tok0, seg, gi, ng) in own_groups:
            xg, xkey = C.xgs.next()
            at, atk = atg.next()
            sm, smk = smg.next()
            P.dma(xg[:], x1s[tok0:tok0 + GT, :].rearrange("(t p) c -> p t c", p=128), [], [xkey], "ld_" + xkey)
            P.dma(at[:], attnT[:, tok0:tok0 + GT].rearrange("(c p) t -> p c t", p=128), [], [atk], "ld_" + atk)
            P.dma(sm[:], ssm_tok[tok0:tok0 + GT, :].rearrange("(t p) c -> p t c", p=128), [], [smk], "ld_" + smk)
            for t in range(4):
                bank, bkey = tr_banks.next()
                bv = bank[:].bitcast(BF16)
                for kc in range(8):
                    P.op("pe", lambda e, t=t, kc=kc, bv=bv, sm=sm: e.transpose(bv[:, kc * 128:(kc + 1) * 128],
                                                                               sm[:, t, kc * 128:(kc + 1) * 128], ident_b),
                         [smk, "cstb"], [bkey])
                P.op("act", lambda e, t=t, bv=bv: e.copy(out=smT[:, :, t * 128:(t + 1) * 128],
                                                         in_=bv.rearrange("p (k c) -> p k c", k=8)), [bkey], [("smT", t)])
            P.op("act", lambda e, at=at: e.activation(out=asq[:], in_=at[:], func=AF.Square), [atk], ["asq"])
            pss, pssk = bk(6)
            for t in range(4):
                for kc in range(8):
                    P.op("pe", lambda e, t=t, kc=kc: e.matmul(pss[:, t:t + 1], lhsT=asq[:, kc, t * 128:(t + 1) * 128],
                                                              rhs=ones_b[:, 0:1], start=(kc == 0), stop=(kc == 7)),
                         ["asq", "cstb"], [pssk])
            P.op("act", lambda e: e.activation(out=ars[:, 0:4], in_=pss[:, 0:4], func=AF.Ln, scale=1.0 / D, bias=EPS), [pssk], ["ars"])
            P.op("act", lambda e: e.activation(out=ars[:, 0:4], in_=ars[:, 0:4], func=AF.Exp, scale=-0.5), ["ars"], ["ars"])
            smT_keys = [("smT", t) for t in range(4)]
            for n in range(2):
                wa, wak = C.wload(wb["wout"][:, 0:8, n * 512:(n + 1) * 512], lambda b: b[:, :, :], ("wb", "wout"))
                ws, wsk = C.wload(wb["wout"][:, 8:16, n * 512:(n + 1) * 512], lambda b: b[:, :, :], ("wb", "wout"))
                for t in range(4):
                    pa, pak = oa_banks.next()
                    po, pok = os_banks.next()
                    for kc in range(8):
                        P.op("pe", lambda e, t=t, kc=kc, pa=pa, wa=wa, at=at: e.matmul(
                            pa[:], lhsT=at[:, kc, t * 128:(t + 1) * 128], rhs=wa[:, kc, :], start=(kc == 0), stop=(kc == 7)),
                            [atk, wak], [pak])
                    for kc in range(8):
                        P.op("pe", lambda e, t=t, kc=kc, po=po, ws=ws: e.matmul(
                            po[:], lhsT=smT[:, kc, t * 128:(t + 1) * 128], rhs=ws[:, kc, :], start=(kc == 0), stop=(kc == 7)),
                            smT_keys + [wsk], [pok])
                    xsl = xg[:, t, n * 512:(n + 1) * 512]
                    P.op("dve", lambda e, t=t, pa=pa, xsl=xsl: e.scalar_tensor_tensor(
                        out=xsl, in0=pa[:], scalar=ars[:, t:t + 1], in1=xsl, op0=ALU.mult, op1=ALU.add), [pak, "ars", xkey], [xkey])
                    P.op("dve", lambda e, po=po, xsl=xsl: e.tensor_tensor(out=xsl, in0=xsl, in1=po[:], op=ALU.add), [pok, xkey], [xkey])
            C.norm_transpose(xg, xkey, 0)
            C.ffn(xg, xkey, "w2g", "w2u", "w2d")
            orow = tok0 if seg == "P" else NPS + (tok0 - S0)
            P.dma(yout[orow:orow + GT, :].rearrange("(t p) c -> p t c", p=128), xg[:], [xkey], [("yout", orow)], "st_" + xkey, q="pool")
    return finish(nc, es, P, sems, None)


def finish(nc, es, P, sems, out_chans):
    chans = sorted({str(o.dma_chan) for o in P.ops if o.dma_chan is not None})
    assert len(chans) <= 97, len(chans)
    sems["dma"] = [nc.alloc_semaphore(name=f"d{i}") for i in range(len(chans))]
    P.build(sems)
    if out_chans is None:
        out_chans = list(P.chan_sem.keys())
    with nc.Block() as block:
        P.emit(block, out_chans=out_chans)
    if es is not None:
        es.close()
    return nc


def make_consts():
    c = np.zeros((128, NCONST), np.float32)
    i = np.arange(128)
    c[:, C_ID:C_ID + 128] = np.eye(128)
    c[:, C_TRIF:C_TRIF + 128] = (i[:, None] <= i[None, :])
    c[:, C_TRIB:C_TRIB + 128] = (i[:, None] >= i[None, :])
    c[:, C_BLK:C_BLK + 128] = ((i[:, None] // 64) == (i[None, :] // 64))
    c[:, C_ANTI:C_ANTI + 128] = ((i[:, None] + i[None, :]) == 127)
    c[:, C_ONES:C_ONES + 128] = 1.0
    return c


def make_vecs(inp):
    v = np.zeros((128, NV), np.float32)

    def colmajor(g):
        return np.asarray(g, np.float32).reshape(8, 128).T

    v[:, V_G1:V_G1 + 8] = colmajor(inp["ffn1_norm_g"][0])
    v[:, V_GMIX:V_GMIX + 8] = colmajor(inp["mix_norm_g"][0])
    v[:, V_G2:V_G2 + 8] = colmajor(inp["ffn2_norm_g"][0])
    v[:, V_GATT:V_GATT + 8] = colmajor(inp["attn_out_g"][0])
    v[:, V_GSSM:V_GSSM + 8] = colmajor(inp["ssm_out_g"][0])
    v[:, V_GQ] = np.tile(np.asarray(inp["q_norm_g"][0], np.float32), 2)
    v[:, V_GK] = np.tile(np.asarray(inp["k_norm_g"][0], np.float32), 2)
    cw = np.asarray(inp["conv_w"][0], np.float32)
    for tap in range(5):
        v[:, V_CW + tap * 16:V_CW + (tap + 1) * 16] = cw[tap].reshape(16, 128).T
    v[:, V_CB:V_CB + 16] = np.asarray(inp["conv_b"][0], np.float32).reshape(16, 128).T
    v[:, V_DTB:V_DTB + 32] = np.asarray(inp["dt_bias"][0], np.float32).reshape(1, 32)
    v[:, V_ALOG:V_ALOG + 32] = np.asarray(inp["a_log"][0], np.float32).reshape(1, 32)
    v[:, V_DSK:V_DSK + 16] = np.asarray(inp["d_skip"][0], np.float32).reshape(1, 16)
    return v


def _t5_bucket(rel):
    nb = 16
    max_exact = 8
    n = np.abs(rel)
    large = max_exact + (np.log(np.maximum(n, 1) / max_exact) / np.log(1024 / max_exact) * (nb - max_exact)).astype(np.int32)
    large = np.minimum(large, nb - 1)
    return (np.where(rel > 0, nb, 0) + np.where(n < max_exact, n, large)).astype(np.int32)


def make_onehot():
    oh = np.zeros((33, 3 * 384), np.float32)
    for p, dil in enumerate((1, 4, 16)):
        j = np.arange(384)
        delta = j - 191
        bkt = _t5_bucket(delta * dil)
        inw = np.abs(delta) <= 64
        for jj in range(384):
            if inw[jj]:
                oh[bkt[jj], p * 384 + jj] = 1.0
            else:
                oh[32, p * 384 + jj] = NEG
    return oh


def make_kval(core):
    kv = np.zeros((128, NKV), np.float32)
    k = np.arange(128)
    for ci, (seg, dil, r, jt) in enumerate(KV_COLS):
        if seg == "P":
            start, slen = 0, NPS
        else:
            start, slen = (core % 4) * TS, 4 * TS
        g = start + r + dil * (128 * jt - 64 + k)
        kv[:, ci] = np.where((g >= 0) & (g < slen), 0.0, NEG)
    return kv


def make_in_maps(inp):
    xp = np.asarray(inp["x_prompt"], np.float32)
    xs = np.asarray(inp["x_sample"], np.float32)
    consts = make_consts()
    vecs = make_vecs(inp)
    shared = {
        "w1g": np.ascontiguousarray(inp["ffn1_w_gate"][0], dtype=np.float32),
        "w1u": np.ascontiguousarray(inp["ffn1_w_up"][0], dtype=np.float32),
        "w1d": np.ascontiguousarray(inp["ffn1_w_down"][0], dtype=np.float32),
        "win": np.ascontiguousarray(inp["w_in"][0], dtype=np.float32),
        "wout": np.ascontiguousarray(inp["w_out"][0], dtype=np.float32),
        "w2g": np.ascontiguousarray(inp["ffn2_w_gate"][0], dtype=np.float32),
        "w2u": np.ascontiguousarray(inp["ffn2_w_up"][0], dtype=np.float32),
        "w2d": np.ascontiguousarray(inp["ffn2_w_down"][0], dtype=np.float32),
        "vecs": vecs, "consts": consts,
        "relb": np.concatenate([np.asarray(inp["rel_bias"], np.float32), np.ones((1, 16), np.float32)], 0),
        "onehot": make_onehot(),
    }
    maps = []
    for c in range(8):
        s, j = c // 4, c % 4
        x = np.zeros((NTOK, D), np.float32)
        x[0:NPS] = xp[c]
        lo = j * TS - HALO
        hi = (j + 1) * TS + HALO
        slo, shi = max(lo, 0), min(hi, xs.shape[1])
        x[NPS + (slo - lo):NPS + (shi - lo)] = xs[s, slo:shi]
        m = dict(shared)
        m["xin"] = x
        sel = np.zeros((128, 4), np.float32)
        sel[:, j] = 1.0
        m["sel"] = sel
        m["kval"] = make_kval(c)
        maps.append(m)
    return maps


_CACHE = {}


def kernel(**inputs):
    if "nc" not in _CACHE:
        _CACHE["nc"] = build_program()
    nc = _CACHE["nc"]
    maps = make_in_maps(inputs)
    res = run_bass_kernel_spmd(nc, maps, core_ids=list(range(8)))
    yp = np.zeros((8, NPS, D), np.float32)
    ys = np.zeros((2, 4 * TS, D), np.float32)
    for c in range(8):
        y = res.results[c]["yout"]
        yp[c] = y[0:NPS]
        ys[c // 4, (c % 4) * TS:(c % 4 + 1) * TS] = y[NPS:NPS + TS]
    return (yp, ys)
```
